# Optimizing a Trainium2 kernel written in Bass

```python
import jax, jax.numpy as jnp
from jax import lax
import numpy as np

D_MODEL = 2048
BATCH = 1
SEQ = 8192
DEPTH = 4

D_MIX = D_MODEL
POOL_WIDTH = D_MIX // 4
POOL_WINDOWS = (2, 4, 8, 16)
POOL_GROUP = POOL_WIDTH // len(POOL_WINDOWS)
CONV_WIDTH = D_MIX // 4
CONV_KERNEL = 31
CONV_PAD = CONV_KERNEL // 2
LRU_WIDTH = D_MIX // 2
LRU_HEADS = 8
LRU_HEAD_DIM = LRU_WIDTH // LRU_HEADS
LRU_CONV = 4
LRU_C = 8.0
D_IN = POOL_WIDTH + 2 * CONV_WIDTH + 2 * LRU_WIDTH
D_FF = 5632
RMS_EPS = 1e-6
LN_EPS = 1e-5

kernel_name = "bidir_hybrid_pool_conv_rglru_macaron"


def rmsnorm(x, g):
    xf = x.astype(jnp.float32)
    y = xf * lax.rsqrt(jnp.mean(xf * xf, axis=-1, keepdims=True) + RMS_EPS)
    return (y * g.astype(jnp.float32)).astype(x.dtype)


def swiglu(x, w_gate, w_up, w_down):
    return (jax.nn.silu(x @ w_gate) * (x @ w_up)) @ w_down


def depthwise_conv(x, w, pad_left, pad_right):
    c = x.shape[-1]
    return lax.conv_general_dilated(
        x, w[:, None, :].astype(x.dtype), window_strides=(1,),
        padding=[(pad_left, pad_right)],
        dimension_numbers=("NWC", "WIO", "NWC"), feature_group_count=c)


def pool_mixer(u, w, scale):
    b, s, _ = u.shape
    uf = u.astype(jnp.float32)
    cs = jnp.pad(jnp.cumsum(uf, axis=1), ((0, 0), (1, 0), (0, 0)))
    t = jnp.arange(s)
    outs = []
    for g, win in enumerate(POOL_WINDOWS):
        lo = jnp.clip(t - win // 2, 0, s)
        hi = jnp.clip(t + win // 2, 0, s)
        csg = cs[..., g * POOL_GROUP:(g + 1) * POOL_GROUP]
        ssum = jnp.take(csg, hi, axis=1) - jnp.take(csg, lo, axis=1)
        cnt = (hi - lo).astype(jnp.float32)[None, :, None]
        outs.append(ssum / cnt)
    pooled = (jnp.concatenate(outs, axis=-1) - uf).astype(u.dtype)
    pooled = pooled.reshape(b, s, len(POOL_WINDOWS), POOL_GROUP)
    mixed = jnp.einsum("bsgi,gij->bsgj", pooled, w).reshape(b, s, POOL_WIDTH)
    return mixed * scale


def conv_module(u, dw_w, dw_b, ln_g, ln_b):
    v, gate = jnp.split(u, 2, axis=-1)
    z = v * jax.nn.sigmoid(gate)
    z = depthwise_conv(z, dw_w, CONV_PAD, CONV_PAD) + dw_b
    zf = z.astype(jnp.float32)
    mu = jnp.mean(zf, axis=-1, keepdims=True)
    var = jnp.mean(jnp.square(zf - mu), axis=-1, keepdims=True)
    zn = ((zf - mu) * lax.rsqrt(var + LN_EPS)).astype(z.dtype) * ln_g + ln_b
    return jax.nn.silu(zn)


def _linear_recurrence(c1, c2):
    a1, b1 = c1
    a2, b2 = c2
    return a1 * a2, a2 * b1 + b2


def rglru_direction(x, conv_w, conv_b, w_a, b_a, w_x, b_x, lam):
    b, s, c = x.shape
    xc = depthwise_conv(x, conv_w, LRU_CONV - 1, 0) + conv_b
    xh = xc.reshape(b, s, LRU_HEADS, LRU_HEAD_DIM)
    r = jax.nn.sigmoid(jnp.einsum("bshi,hij->bshj", xh, w_a).reshape(b, s, c) + b_a)
    i = jax.nn.sigmoid(jnp.einsum("bshi,hij->bshj", xh, w_x).reshape(b, s, c) + b_x)
    log_a = -LRU_C * r.astype(jnp.float32) * jax.nn.softplus(-lam.astype(jnp.float32))
    a = jnp.exp(log_a)
    mult = jnp.sqrt(-jnp.expm1(2.0 * log_a))
    bterm = mult * (i * xc).astype(jnp.float32)
    _, h = lax.associative_scan(_linear_recurrence, (a, bterm), axis=1)
    return h.astype(x.dtype)


def rglru_mixer(u, conv_w, conv_b, w_a, b_a, w_x, b_x, lam):
    gate = jax.nn.gelu(u[..., :LRU_WIDTH])
    xr = u[..., LRU_WIDTH:]
    h_fwd = rglru_direction(xr, conv_w[0], conv_b[0], w_a[0], b_a[0], w_x[0], b_x[0], lam[0])
    h_bwd = rglru_direction(xr[:, ::-1], conv_w[1], conv_b[1], w_a[1], b_a[1], w_x[1], b_x[1], lam[1])[:, ::-1]
    return (h_fwd + h_bwd) * gate


def setup_inputs(seed: int = 0) -> dict:
    key = jax.random.key(seed)
    ks = jax.random.split(key, 32)
    L = DEPTH
    f32 = jnp.float32

    def nrm(k, shape, fan_in):
        return jax.random.normal(k, shape, f32) * (fan_in ** -0.5)

    def gain(k, shape):
        return 1.0 + 0.1 * jax.random.normal(k, shape, f32)

    def bias(k, shape):
        return 0.02 * jax.random.normal(k, shape, f32)

    a8 = jax.random.uniform(ks[31], (L, 2, LRU_WIDTH), f32, minval=0.9, maxval=0.999)
    a_base = a8 ** (1.0 / LRU_C)
    lru_lambda = jnp.log(a_base) - jnp.log1p(-a_base)

    return {
        "x": jax.random.normal(ks[0], (BATCH, SEQ, D_MODEL), f32),
        "norm_ffn1": gain(ks[1], (L, D_MODEL)),
        "ffn1_w_gate": nrm(ks[2], (L, D_MODEL, D_FF), D_MODEL),
        "ffn1_w_up": nrm(ks[3], (L, D_MODEL, D_FF), D_MODEL),
        "ffn1_w_down": nrm(ks[4], (L, D_FF, D_MODEL), D_FF),
        "norm_mix": gain(ks[5], (L, D_MODEL)),
        "w_in": nrm(ks[6], (L, D_MODEL, D_IN), D_MODEL),
        "pool_w": nrm(ks[7], (L, len(POOL_WINDOWS), POOL_GROUP, POOL_GROUP), POOL_GROUP),
        "pool_scale": gain(ks[8], (L, POOL_WIDTH)),
        "conv_dw_w": nrm(ks[9], (L, CONV_KERNEL, CONV_WIDTH), CONV_KERNEL),
        "conv_dw_b": bias(ks[10], (L, CONV_WIDTH)),
        "conv_ln_g": gain(ks[11], (L, CONV_WIDTH)),
        "conv_ln_b": bias(ks[12], (L, CONV_WIDTH)),
        "lru_conv_w": nrm(ks[13], (L, 2, LRU_CONV, LRU_WIDTH), LRU_CONV),
        "lru_conv_b": bias(ks[14], (L, 2, LRU_WIDTH)),
        "lru_w_a": nrm(ks[15], (L, 2, LRU_HEADS, LRU_HEAD_DIM, LRU_HEAD_DIM), LRU_HEAD_DIM),
        "lru_b_a": bias(ks[16], (L, 2, LRU_WIDTH)),
        "lru_w_x": nrm(ks[17], (L, 2, LRU_HEADS, LRU_HEAD_DIM, LRU_HEAD_DIM), LRU_HEAD_DIM),
        "lru_b_x": bias(ks[18], (L, 2, LRU_WIDTH)),
        "lru_lambda": lru_lambda,
        "w_out": nrm(ks[19], (L, D_MIX, D_MODEL), D_MIX),
        "norm_ffn2": gain(ks[20], (L, D_MODEL)),
        "ffn2_w_gate": nrm(ks[21], (L, D_MODEL, D_FF), D_MODEL),
        "ffn2_w_up": nrm(ks[22], (L, D_MODEL, D_FF), D_MODEL),
        "ffn2_w_down": nrm(ks[23], (L, D_FF, D_MODEL), D_FF),
        "norm_final": gain(ks[24], (D_MODEL,)),
    }


def reference(x, norm_ffn1, ffn1_w_gate, ffn1_w_up, ffn1_w_down, norm_mix, w_in,
              pool_w, pool_scale, conv_dw_w, conv_dw_b, conv_ln_g, conv_ln_b,
              lru_conv_w, lru_conv_b, lru_w_a, lru_b_a, lru_w_x, lru_b_x, lru_lambda,
              w_out, norm_ffn2, ffn2_w_gate, ffn2_w_up, ffn2_w_down, norm_final):
    split_conv = POOL_WIDTH
    split_lru = POOL_WIDTH + 2 * CONV_WIDTH
    for l in range(DEPTH):
        h = rmsnorm(x, norm_ffn1[l])
        x = x + 0.5 * swiglu(h, ffn1_w_gate[l], ffn1_w_up[l], ffn1_w_down[l])

        h = rmsnorm(x, norm_mix[l])
        u = h @ w_in[l]
        u_pool = u[..., :split_conv]
        u_conv = u[..., split_conv:split_lru]
        u_lru = u[..., split_lru:]
        y_pool = pool_mixer(u_pool, pool_w[l], pool_scale[l])
        y_conv = conv_module(u_conv, conv_dw_w[l], conv_dw_b[l], conv_ln_g[l], conv_ln_b[l])
        y_lru = rglru_mixer(u_lru, lru_conv_w[l], lru_conv_b[l], lru_w_a[l], lru_b_a[l],
                            lru_w_x[l], lru_b_x[l], lru_lambda[l])
        y = jnp.concatenate([y_pool, y_conv, y_lru], axis=-1)
        x = x + y @ w_out[l]

        h = rmsnorm(x, norm_ffn2[l])
        x = x + 0.5 * swiglu(h, ffn2_w_gate[l], ffn2_w_up[l], ffn2_w_down[l])
    return rmsnorm(x, norm_final)
```

```python
import numpy as np
from contextlib import ExitStack
import concourse.bass as bass
import concourse.mybir as mybir
from concourse.bass_utils import run_bass_kernel_spmd

F32 = mybir.dt.float32
BF16 = mybir.dt.bfloat16
AF = mybir.ActivationFunctionType
ALU = mybir.AluOpType

ENGS = ("pe", "act", "dve", "pool", "sp")
NRING = 6
SAME_ENG_DIST = 4

NCORES = 8
D = 2048
DC = 16
DFF = 5632
FC = 44
DIN = 3584
SEQ = 8192
TOWN = 1024
HALO = 16
TA = TOWN + 2 * HALO
DEPTH = 4
RMS_EPS = 1e-6
LN_EPS = 1e-5


class _Op:
    __slots__ = ("id", "eng", "fn", "deps", "dma", "seq", "pos", "ring", "rval", "ndep")


class Sched:
    def __init__(self):
        self.ops = []
        self.by_eng = {e: [] for e in ENGS}
        self.last_w = {}
        self.readers = {}
        self.pending_bar = {}
        self.dma_since_bar = []

    def op(self, eng, fn, reads=(), writes=(), dma=False):
        o = _Op()
        o.id = len(self.ops)
        o.eng = eng
        o.fn = fn
        o.dma = dma
        o.seq = None
        o.ndep = 0
        deps = set()
        for r in reads:
            w = self.last_w.get(r)
            if w is not None:
                deps.add(w)
        for k in writes:
            w = self.last_w.get(k)
            if w is not None:
                deps.add(w)
            rd = self.readers.get(k)
            if rd:
                deps.update(rd.values())
        b = self.pending_bar.pop(eng, None)
        if b:
            deps.update(b)
        o.deps = deps
        for r in reads:
            d = self.readers.setdefault(r, {})
            d[("dma", o.id) if dma else eng] = o.id
        for k in writes:
            self.last_w[k] = o.id
            self.readers[k] = {}
        o.pos = len(self.by_eng[eng])
        self.ops.append(o)
        self.by_eng[eng].append(o)
        if dma:
            self.dma_since_bar.append(o.id)
        return o

    def barrier(self):
        b = set(self.dma_since_bar)
        for e in ENGS:
            if self.by_eng[e]:
                b.add(self.by_eng[e][-1].id)
        for e in ENGS:
            self.pending_bar.setdefault(e, set()).update(b)
        self.dma_since_bar = []

    def prepare(self):
        ops = self.ops
        for o in ops:
            for d in o.deps:
                p = ops[d]
                if p.dma:
                    continue
                if p.eng == o.eng and (p.eng == "pe" or o.pos - p.pos >= SAME_ENG_DIST):
                    continue
                p.ndep += 1
        for e in ENGS:
            s = 0
            k = 0
            for o in self.by_eng[e]:
                if o.dma:
                    o.ring = k % NRING
                    o.rval = 16 * (k // NRING + 1)
                    k += 1
                elif o.ndep:
                    s += 1
                    o.seq = s

    def run(self, e, h, sems, rings):
        ops = self.ops
        waited = {}
        for o in self.by_eng[e]:
            need = {}
            for d in o.deps:
                p = ops[d]
                if p.dma:
                    sem = rings[p.eng][p.ring]
                    val = p.rval
                else:
                    if p.seq is None:
                        continue
                    if p.eng == e and (e == "pe" or o.pos - p.pos >= SAME_ENG_DIST):
                        continue
                    sem = sems[p.eng]
                    val = p.seq
                k = id(sem)
                if k not in need or need[k][1] < val:
                    need[k] = (sem, val)
            if o.dma and o.rval > 16:
                sem = rings[e][o.ring]
                k = id(sem)
                v = o.rval - 16
                if k not in need or need[k][1] < v:
                    need[k] = (sem, v)
            for k, (sem, val) in need.items():
                if waited.get(k, 0) >= val:
                    continue
                waited[k] = val
                h.wait_ge(sem, val)
            ins = o.fn(h)
            if o.dma:
                ins.then_inc(rings[e][o.ring], 16)
            elif o.seq is not None:
                ins.then_inc(sems[e], 1)
        last = {}
        for o in self.by_eng[e]:
            if o.dma:
                last[o.ring] = o.rval
        for r, v in last.items():
            h.wait_ge(rings[e][r], v)


class Ctx:
    def __init__(self, nc, es, arena_bytes=200 * 1024):
        self.nc = nc
        self.S = Sched()
        self.arena = es.enter_context(nc.sbuf_tensor("arena", [128, arena_bytes // 2], BF16))
        self.arena_bytes = arena_bytes
        self.off = 0
        self.ps = [es.enter_context(nc.psum_tensor("ps%d" % i, [128, 512], F32)) for i in range(8)]
        self.psi = 0
        self.uid = 0

    def alloc(self, name, free_shape, dt):
        n = int(np.prod(free_shape))
        esz = 2 if dt == BF16 else 4
        nbytes = (n * esz + 63) // 64 * 64
        assert self.off + nbytes <= self.arena_bytes, (name, self.off, nbytes, self.arena_bytes)
        v = self.arena[:, self.off // 2:(self.off + n * esz) // 2]
        if dt != BF16:
            v = v.bitcast(dt)
        if len(free_shape) == 2:
            v = v.rearrange("p (a b) -> p a b", a=free_shape[0])
        elif len(free_shape) == 3:
            v = v.rearrange("p (a b c) -> p a b c", a=free_shape[0], b=free_shape[1])
        self.off += nbytes
        return v

    def mark(self):
        return self.off

    def release(self, m):
        self.S.barrier()
        self.off = m

    def next_ps(self):
        i = self.psi
        self.psi = (self.psi + 1) % 8
        return i

    def key(self, base):
        self.uid += 1
        return "%s#%d" % (base, self.uid)


def tiles_of(T):
    out = []
    t = 0
    while t < T:
        n = min(512, T - t)
        out.append((t, n))
        t += n
    return out


def emit_load_x(C, x_sb, x_dram, T):
    S = C.S
    for c in range(0, DC, 4):
        S.op("sp", lambda h, c=c: h.dma_start(out=x_sb[:, c:c + 4, :], in_=x_dram[:, c:c + 4, :]),
             writes=[("x", cc) for cc in range(c, c + 4)], dma=True)


def emit_rmsnorm(C, x_sb, g_sb, gcol0, h_sb, T, ones_f, hkey="h", out_f32=None):
    S = C.S
    m = C.mark()
    NSQ = 4
    sq = [C.alloc("sq%d" % i, [512], BF16) for i in range(NSQ)]
    ones_b = C.alloc("ones_b", [128], BF16)
    S.op("dve", lambda h: h.memset(ones_b[:], 1.0), writes=["ones_b"])
    rstd = C.alloc("rstd", [T], F32)
    epsc = C.alloc("epsc", [1], F32)
    S.op("dve", lambda h: h.memset(epsc[:], RMS_EPS), writes=["epsc"])
    for (t0, n) in tiles_of(T):
        pi = C.next_ps()
        ps = C.ps[pi]
        for c in range(DC):
            i = c % NSQ
            eng = "act" if c % 2 == 0 else "dve"
            if eng == "act":
                S.op("act", lambda h, c=c, i=i, t0=t0, n=n: h.activation(out=sq[i][:, 0:n], in_=x_sb[:, c, t0:t0 + n], func=AF.Square),
                     reads=[("x", c)], writes=[("sq", i)])
            else:
                S.op("dve", lambda h, c=c, i=i, t0=t0, n=n: h.tensor_tensor(out=sq[i][:, 0:n], in0=x_sb[:, c, t0:t0 + n],
                                                                  in1=x_sb[:, c, t0:t0 + n], op=ALU.mult),
                     reads=[("x", c)], writes=[("sq", i)])
            S.op("pe", lambda h, c=c, i=i, ps=ps, n=n: h.matmul(ps[:, 0:n], lhsT=ones_b[:], rhs=sq[i][:, 0:n],
                                                     start=(c == 0), stop=(c == DC - 1)),
                 reads=[("sq", i), "ones_b"], writes=[("ps", pi)])
        S.op("act", lambda h, t0=t0, n=n, ps=ps: h.activation(out=rstd[:, t0:t0 + n], in_=ps[:, 0:n], func=AF.Sqrt, scale=1.0 / D,
                                                               bias=epsc[:, 0:1]),
             reads=[("ps", pi), "epsc"], writes=[("rstd", t0)])
        S.op("dve", lambda h, t0=t0, n=n: h.reciprocal(out=rstd[:, t0:t0 + n], in_=rstd[:, t0:t0 + n]),
             reads=[("rstd", t0)], writes=[("rstd", t0)])
        for c in range(DC):
            eng = "dve"
            dst = (h_sb[:, c, t0:t0 + n] if out_f32 is None else out_f32[:, c, t0:t0 + n])
            S.op(eng, lambda h, c=c, dst=dst, t0=t0, n=n: h.scalar_tensor_tensor(out=dst, in0=x_sb[:, c, t0:t0 + n],
                                                                      scalar=g_sb[:, gcol0 + c:gcol0 + c + 1],
                                                                      in1=rstd[:, t0:t0 + n], op0=ALU.mult, op1=ALU.mult),
                 reads=[("x", c), ("rstd", t0), "consts"], writes=[(hkey, c)])
    C.release(m)


def emit_ffn(C, x_sb, h_sb, T, wg, wu, wd):
    S = C.S
    TT = tiles_of(T)
    NPH = 4
    JP = FC // NPH
    m = C.mark()
    hid = C.alloc("hid", [JP, T], BF16)
    NWR = 3
    wgb = [C.alloc("wgb%d" % i, [DC, 128], BF16) for i in range(NWR)]
    wub = [C.alloc("wub%d" % i, [DC, 128], BF16) for i in range(NWR)]
    wdb = [C.alloc("wdb%d" % i, [JP, 256], BF16) for i in range(2)]
    sg = [C.alloc("sg%d" % i, [512], F32) for i in range(3)]
    wgv = wg.rearrange("(kc p) n -> p kc n", p=128)
    wuv = wu.rearrange("(kc p) n -> p kc n", p=128)
    wdv = wd.rearrange("(j p) n -> p j n", p=128)
    wcnt = 0
    scnt = 0
    dcnt = 0
    for ph in range(NPH):
        for jj in range(JP):
            j = ph * JP + jj
            wi = wcnt % NWR
            wcnt += 1
            S.op("pool", lambda h, j=j, wi=wi: h.dma_start(out=wgb[wi][:], in_=wgv[:, :, j * 128:(j + 1) * 128]),
                 writes=[("wgb", wi)], dma=True)
            S.op("pool", lambda h, j=j, wi=wi: h.dma_start(out=wub[wi][:], in_=wuv[:, :, j * 128:(j + 1) * 128]),
                 writes=[("wub", wi)], dma=True)
            for (t0, n) in TT:
                pg = C.next_ps()
                pu = C.next_ps()

                def mm_g(h, wi=wi, pg=pg, t0=t0, n=n):
                    for kc in range(DC):
                        ins = h.matmul(C.ps[pg][:, 0:n], lhsT=wgb[wi][:, kc, :], rhs=h_sb[:, kc, t0:t0 + n],
                                       start=(kc == 0), stop=(kc == DC - 1))
                    return ins

                def mm_u(h, wi=wi, pu=pu, t0=t0, n=n):
                    for kc in range(DC):
                        ins = h.matmul(C.ps[pu][:, 0:n], lhsT=wub[wi][:, kc, :], rhs=h_sb[:, kc, t0:t0 + n],
                                       start=(kc == 0), stop=(kc == DC - 1))
                    return ins

                hreads = [("h", c) for c in range(DC)]
                S.op("pe", mm_g, reads=[("wgb", wi)] + hreads, writes=[("ps", pg)])
                S.op("pe", mm_u, reads=[("wub", wi)] + hreads, writes=[("ps", pu)])
                si = scnt % 3
                scnt += 1
                S.op("act", lambda h, si=si, pg=pg, n=n: h.activation(out=sg[si][:, 0:n], in_=C.ps[pg][:, 0:n], func=AF.Silu),
                     reads=[("ps", pg)], writes=[("sg", si)])
                S.op("dve", lambda h, si=si, pu=pu, jj=jj, t0=t0, n=n: h.tensor_tensor(
                    out=hid[:, jj, t0:t0 + n], in0=sg[si][:, 0:n], in1=C.ps[pu][:, 0:n], op=ALU.mult),
                    reads=[("sg", si), ("ps", pu)], writes=[("hid", jj, t0)])
        for n2 in range(DC // 2):
            di = dcnt % 2
            dcnt += 1
            S.op("pool", lambda h, di=di, n2=n2, ph=ph: h.dma_start(
                out=wdb[di][:], in_=wdv[:, ph * JP:(ph + 1) * JP, n2 * 256:(n2 + 1) * 256]),
                writes=[("wdb", di)], dma=True)
            for nn in range(2):
                nchunk = n2 * 2 + nn
                for (t0, n) in TT:
                    po = C.next_ps()

                    def mm_d(h, di=di, nn=nn, po=po, t0=t0, n=n):
                        for jj in range(JP):
                            ins = h.matmul(C.ps[po][:, 0:n], lhsT=wdb[di][:, jj, nn * 128:(nn + 1) * 128],
                                           rhs=hid[:, jj, t0:t0 + n], start=(jj == 0), stop=(jj == JP - 1))
                        return ins

                    S.op("pe", mm_d, reads=[("wdb", di)] + [("hid", jj, t0) for jj in range(JP)], writes=[("ps", po)])
                    S.op("dve", lambda h, po=po, nchunk=nchunk, t0=t0, n=n: h.scalar_tensor_tensor(
                        out=x_sb[:, nchunk, t0:t0 + n], in0=C.ps[po][:, 0:n], scalar=0.5, in1=x_sb[:, nchunk, t0:t0 + n],
                        op0=ALU.mult, op1=ALU.add),
                        reads=[("ps", po), ("x", nchunk)], writes=[("x", nchunk)])
    C.release(m)


def emit_softplus_c(C, lam_sb, c1, c2, ncols):
    S = C.S
    m = C.mark()
    e = C.alloc("sp_e", [ncols], F32)
    s = C.alloc("sp_s", [ncols], F32)
    s2 = C.alloc("sp_s2", [ncols], F32)
    acc = C.alloc("sp_acc", [ncols], F32)
    S.op("act", lambda h: h.activation(out=e[:], in_=lam_sb, func=AF.Exp, scale=-1.0), reads=["consts"], writes=["sp_e"])
    S.op("dve", lambda h: h.tensor_scalar(out=s[:], in0=e[:], scalar1=2.0, scalar2=None, op0=ALU.add),
         reads=["sp_e"], writes=["sp_s"])
    S.op("dve", lambda h: h.reciprocal(out=s[:], in_=s[:]), reads=["sp_s"], writes=["sp_s"])
    S.op("dve", lambda h: h.tensor_tensor(out=s[:], in0=e[:], in1=s[:], op=ALU.mult), reads=["sp_e", "sp_s"], writes=["sp_s"])
    S.op("dve", lambda h: h.tensor_tensor(out=s2[:], in0=s[:], in1=s[:], op=ALU.mult), reads=["sp_s"], writes=["sp_s2"])
    NT = 9
    S.op("dve", lambda h: h.memset(acc[:], 1.0 / (2 * NT + 1)), writes=["sp_acc"])
    for k in range(NT - 1, -1, -1):
        S.op("dve", lambda h, k=k: h.tensor_tensor(out=acc[:], in0=acc[:], in1=s2[:], op=ALU.mult),
             reads=["sp_acc", "sp_s2"], writes=["sp_acc"])
        S.op("dve", lambda h, k=k: h.tensor_scalar(out=acc[:], in0=acc[:], scalar1=1.0 / (2 * k + 1), scalar2=None, op0=ALU.add),
             reads=["sp_acc"], writes=["sp_acc"])
    S.op("dve", lambda h: h.scalar_tensor_tensor(out=c1, in0=acc[:], scalar=-16.0, in1=s[:], op0=ALU.mult, op1=ALU.mult),
         reads=["sp_acc", "sp_s"], writes=["c1"])
    S.op("dve", lambda h: h.tensor_scalar(out=c2, in0=c1, scalar1=2.0, scalar2=None, op0=ALU.mult),
         reads=["c1"], writes=["c2"])
    C.release(m)


def emit_lru_head(C, c, d, xr, xr_key, XO, prm, tmp, tmpb, init_ap, out_h, out_key, acc_r):
    S = C.S
    (xc, kxc), (r, kr), (ig, kig), (a, ka), (bt, kbt) = tmp
    col = d * 8 + c
    cw = prm["lcw"]

    def xs(k):
        off = (k - 3) if d == 0 else (3 - k)
        return xr[:, XO + off:XO + off + TOWN]

    S.op("dve", lambda h: h.tensor_scalar(out=xc[:, 0:TOWN], in0=xs(0), scalar1=cw[:, d, c, 0:1], scalar2=prm["lcb"][:, col:col + 1],
                                           op0=ALU.mult, op1=ALU.add),
         reads=[xr_key, "consts"], writes=[kxc])
    for k in range(1, 4):
        S.op("dve", lambda h, k=k: h.scalar_tensor_tensor(out=xc[:, 0:TOWN], in0=xs(k), scalar=cw[:, d, c, k:k + 1], in1=xc[:, 0:TOWN],
                                                           op0=ALU.mult, op1=ALU.add),
             reads=[xr_key, kxc, "consts"], writes=[kxc])
    S.op("pool", lambda h: h.tensor_copy(out=tmpb[:], in_=xc[:, 0:TOWN]), reads=[kxc], writes=["tmpb"])
    for ti, (t0, n) in enumerate(tiles_of(TOWN)):
        pa = C.next_ps()
        px = C.next_ps()
        S.op("pe", lambda h, pa=pa, t0=t0, n=n: h.matmul(C.ps[pa][:, 0:n], lhsT=prm["wa"][:, d, c, :], rhs=tmpb[:, t0:t0 + n],
                                                          start=True, stop=True),
             reads=["tmpb", "lruw"], writes=[("ps", pa)])
        S.op("pe", lambda h, px=px, t0=t0, n=n: h.matmul(C.ps[px][:, 0:n], lhsT=prm["wx"][:, d, c, :], rhs=tmpb[:, t0:t0 + n],
                                                          start=True, stop=True),
             reads=["tmpb", "lruw"], writes=[("ps", px)])
        if acc_r is not None:
            S.op("act", lambda h, pa=pa, t0=t0, n=n, ti=ti: h.activation(out=r[:, t0:t0 + n], in_=C.ps[pa][:, 0:n], func=AF.Sigmoid,
                                                                          bias=prm["ba"][:, col:col + 1], accum_out=acc_r[:, ti:ti + 1]),
                 reads=[("ps", pa), "consts"], writes=[kr, "accr"])
        else:
            S.op("act", lambda h, pa=pa, t0=t0, n=n: h.activation(out=r[:, t0:t0 + n], in_=C.ps[pa][:, 0:n], func=AF.Sigmoid,
                                                                   bias=prm["ba"][:, col:col + 1]),
                 reads=[("ps", pa), "consts"], writes=[kr])
        S.op("act", lambda h, px=px, t0=t0, n=n: h.activation(out=ig[:, t0:t0 + n], in_=C.ps[px][:, 0:n], func=AF.Sigmoid,
                                                               bias=prm["bx"][:, col:col + 1]),
             reads=[("ps", px), "consts"], writes=[kig])
    S.op("act", lambda h: h.activation(out=a[:, 0:TOWN], in_=r[:, 0:TOWN], func=AF.Exp, scale=prm["c1"][:, col:col + 1]),
         reads=[kr, "c1"], writes=[ka])
    S.op("act", lambda h: h.activation(out=bt[:, 0:TOWN], in_=r[:, 0:TOWN], func=AF.Exp, scale=prm["c2"][:, col:col + 1]),
         reads=[kr, "c2"], writes=[kbt])
    S.op("act", lambda h: h.activation(out=bt[:, 0:TOWN], in_=bt[:, 0:TOWN], func=AF.Sqrt, scale=-1.0, bias=prm["one"][:, 0:1]),
         reads=[kbt, "consts"], writes=[kbt])
    S.op("dve", lambda h: h.tensor_tensor(out=ig[:, 0:TOWN], in0=ig[:, 0:TOWN], in1=xc[:, 0:TOWN], op=ALU.mult), reads=[kig, kxc], writes=[kig])
    S.op("dve", lambda h: h.tensor_tensor(out=bt[:, 0:TOWN], in0=bt[:, 0:TOWN], in1=ig[:, 0:TOWN], op=ALU.mult), reads=[kbt, kig], writes=[kbt])
    if d == 0:
        S.op("dve", lambda h: h.tensor_tensor_scan(out=out_h, data0=a[:, 0:TOWN], data1=bt[:, 0:TOWN], initial=init_ap,
                                                    op0=ALU.mult, op1=ALU.add),
             reads=[ka, kbt, "carry"], writes=[out_key])
    else:
        S.op("dve", lambda h: h.tensor_tensor_scan(out=out_h[:, ::-1], data0=a[:, TOWN - 1::-1], data1=bt[:, TOWN - 1::-1], initial=init_ap,
                                                    op0=ALU.mult, op1=ALU.add),
             reads=[ka, kbt, "carry"], writes=[out_key])


def load_lru_params(C, prm_d):
    S = C.S
    prm = {}
    prm["lcw"] = C.alloc("lcw", [2, 8, 4], F32)
    for nm in ("lcb", "ba", "bx", "lam", "c1", "c2"):
        prm[nm] = C.alloc(nm, [16], F32)
    prm["one"] = C.alloc("one", [1], F32)
    prm["wa"] = C.alloc("wa", [2, 8, 128], BF16)
    prm["wx"] = C.alloc("wx", [2, 8, 128], BF16)
    S.op("dve", lambda h: h.memset(prm["one"][:], 1.0), writes=["one"])
    S.op("sp", lambda h: h.dma_start(out=prm["lcw"][:], in_=prm_d["lru_conv_w"]), writes=["consts"], dma=True)
    for nm, key in (("lcb", "lru_conv_b"), ("ba", "lru_b_a"), ("bx", "lru_b_x"), ("lam", "lru_lambda")):
        S.op("sp", lambda h, nm=nm, key=key: h.dma_start(out=prm[nm][:], in_=prm_d[key]), writes=["consts"], dma=True)
    S.op("pool", lambda h: h.dma_start(out=prm["wa"][:], in_=prm_d["lru_w_a"]), writes=["lruw"], dma=True)
    S.op("pool", lambda h: h.dma_start(out=prm["wx"][:], in_=prm_d["lru_w_x"]), writes=["lruw"], dma=True)
    emit_softplus_c(C, prm["lam"][:], prm["c1"][:], prm["c2"][:], 16)
    return prm


def _finish(nc, C, block, sems, rings):
    S = C.S
    S.prepare()

    @block.tensor
    def _(h):
        S.run("pe", h, sems, rings)

    @block.scalar
    def _(h):
        S.run("act", h, sems, rings)

    @block.vector
    def _(h):
        S.run("dve", h, sems, rings)

    @block.gpsimd
    def _(h):
        S.run("pool", h, sems, rings)

    @block.sync
    def _(h):
        S.run("sp", h, sems, rings)


def _lru_dram(din):
    return dict(lru_conv_w=din("lru_conv_w", [128, 2, 8, 4]), lru_conv_b=din("lru_conv_b", [128, 16]),
                lru_b_a=din("lru_b_a", [128, 16]), lru_b_x=din("lru_b_x", [128, 16]), lru_lambda=din("lru_lambda", [128, 16]),
                lru_w_a=din("lru_w_a", [128, 2, 8, 128]), lru_w_x=din("lru_w_x", [128, 2, 8, 128]))


def build_A():
    nc = bass.Bass("TRN2", target_bir_lowering=False)

    def din(name, shape, dt=F32):
        return nc.dram_tensor(name, shape, dt, kind="ExternalInput").ap()

    def dout(name, shape, dt=F32):
        return nc.dram_tensor(name, shape, dt, kind="ExternalOutput").ap()

    xin = din("xin", [128, DC, TA])
    vmask = din("vmask", [128, TA])
    norms = din("norms", [128, 2 * DC])
    wg = din("wg", [D, DFF]); wu = din("wu", [D, DFF]); wd = din("wd", [DFF, D])
    w_in = din("w_in", [D, DIN])
    pool_w = din("pool_w", [128, 4, 128])
    smallp = din("smallp", [128, 20 + 4 * 31])
    identd = din("ident", [128, 128])
    lru_d = _lru_dram(din)
    xout = dout("xout", [128, DC, TOWN])
    ypc = dout("ypc", [128, 8, TOWN])
    gout = dout("gout", [128, 8, TOWN])
    xrout = dout("xrout", [128, 8, TA])
    car = dout("car", [128, 32])

    with ExitStack() as es:
        C = Ctx(nc, es)
        S = C.S
        sems = {e: es.enter_context(nc.semaphore("s_" + e)) for e in ENGS}
        rings = {e: [es.enter_context(nc.semaphore("r_%s%d" % (e, i))) for i in range(NRING)] for e in ("sp", "pool")}
        block = es.enter_context(nc.Block())

        h_sb = C.alloc("h", [DC, TA], BF16)
        g_sb = C.alloc("g", [2 * DC], F32)
        ones_f = C.alloc("ones_f", [128], F32)
        mx = C.mark()
        x_sb = C.alloc("x", [DC, TA], F32)
        S.op("dve", lambda h: h.memset(ones_f[:], 1.0), writes=["ones"])
        S.op("sp", lambda h: h.dma_start(out=g_sb[:], in_=norms), writes=["consts"], dma=True)
        emit_load_x(C, x_sb, xin, TA)
        emit_rmsnorm(C, x_sb, g_sb, 0, h_sb, TA, ones_f)
        emit_ffn(C, x_sb, h_sb, TA, wg, wu, wd)
        for c in range(0, DC, 4):
            S.op("sp", lambda h, c=c: h.dma_start(out=xout[:, c:c + 4, :], in_=x_sb[:, c:c + 4, HALO:HALO + TOWN]),
                 reads=[("x", cc) for cc in range(c, c + 4)], dma=True)
        emit_rmsnorm(C, x_sb, g_sb, DC, h_sb, TA, ones_f)
        C.release(mx)

        sp_sb = C.alloc("smallp", [20 + 4 * 31], F32)
        S.op("sp", lambda h: h.dma_start(out=sp_sb[:], in_=smallp), writes=["consts"], dma=True)
        pw_sb = C.alloc("pool_w", [4, 128], BF16)
        S.op("pool", lambda h: h.dma_start(out=pw_sb[:], in_=pool_w), writes=["poolw"], dma=True)
        prm = load_lru_params(C, lru_d)
        vm = C.alloc("vm", [TA], F32)
        S.op("sp", lambda h: h.dma_start(out=vm[:], in_=vmask), writes=["vm"], dma=True)
        ones_t = C.alloc("ones_t", [TA], F32)
        S.op("pool", lambda h: h.memset(ones_t[:], 1.0), writes=["ones_t"])
        csm = C.alloc("csm", [TA], F32)
        csu = C.alloc("csu", [TA], F32)
        S.op("dve", lambda h: h.tensor_tensor_scan(out=csm[:], data0=ones_t[:], data1=vm[:], initial=0.0, op0=ALU.mult, op1=ALU.add),
             reads=["ones_t", "vm"], writes=["csm"])
        ident = C.alloc("ident", [128], BF16)
        identf = C.alloc("identf", [128], F32)
        S.op("sp", lambda h: h.dma_start(out=identf[:], in_=identd), writes=["identf"], dma=True)
        S.op("pool", lambda h: h.tensor_copy(out=ident[:], in_=identf[:]), reads=["identf"], writes=["ident"])

        NWR = 2
        wib = [C.alloc("wib%d" % i, [DC, 256], BF16) for i in range(NWR)]
        NU = 4
        ub = [C.alloc("ub%d" % i, [TA], F32) for i in range(NU)]
        tmpf = [C.alloc("tf%d" % i, [TA], F32) for i in range(6)]
        tmpb = C.alloc("tb", [TOWN], BF16)
        zb = C.alloc("zb", [4, TA], BF16)
        z2 = C.alloc("z2", [4, TOWN], F32)
        diag = C.alloc("diag", [31, 128], BF16)
        stg = [C.alloc("stg%d" % i, [TOWN], F32) for i in range(2)]
        accr = C.alloc("accr", [16, 2], F32)
        car_sb = C.alloc("car_sb", [32], F32)
        w_inv = w_in.rearrange("(kc p) n -> p kc n", p=128)
        TT = tiles_of(TA)
        st = {"w": 0, "u": 0, "stg": 0}

        def proj_pair(cols):
            wi = st["w"] % NWR
            st["w"] += 1
            for q, cc in enumerate(cols):
                S.op("pool", lambda h, q=q, cc=cc, wi=wi: h.dma_start(out=wib[wi][:, :, q * 128:(q + 1) * 128],
                                                                     in_=w_inv[:, :, cc * 128:(cc + 1) * 128]),
                     writes=[("wib", wi, q)], dma=True)
            res = []
            for q, cc in enumerate(cols):
                ui = st["u"] % NU
                st["u"] += 1
                res.append(ui)
                for (t0, n) in TT:
                    pi = C.next_ps()

                    def mm(h, q=q, wi=wi, pi=pi, t0=t0, n=n):
                        for kc in range(DC):
                            ins = h.matmul(C.ps[pi][:, 0:n], lhsT=wib[wi][:, kc, q * 128:(q + 1) * 128], rhs=h_sb[:, kc, t0:t0 + n],
                                           start=(kc == 0), stop=(kc == DC - 1))
                        return ins

                    S.op("pe", mm, reads=[("wib", wi, q)] + [("h", c) for c in range(DC)], writes=[("ps", pi)])
                    S.op("act", lambda h, ui=ui, pi=pi, t0=t0, n=n: h.activation(out=ub[ui][:, t0:t0 + n], in_=C.ps[pi][:, 0:n], func=AF.Copy),
                         reads=[("ps", pi)], writes=[("ub", ui)])
            return res

        def store(dst_ap, src_key, src_ap):
            S.op("sp", lambda h: h.dma_start(out=dst_ap, in_=src_ap), reads=[src_key], dma=True)

        WINS = (2, 4, 8, 16)
        for gp in range(2):
            uis = proj_pair([2 * gp, 2 * gp + 1])
            for q in range(2):
                g = 2 * gp + q
                w2 = WINS[g] // 2
                ui = uis[q]
                ssum, rc = tmpf[1], tmpf[2]
                S.op("dve", lambda h, ui=ui: h.tensor_tensor_scan(out=csu[:], data0=ones_t[:], data1=ub[ui][:], initial=0.0,
                                                                   op0=ALU.mult, op1=ALU.add),
                     reads=["ones_t", ("ub", ui)], writes=["csu"])
                lo = HALO - w2 - 1
                hi = HALO + w2 - 1
                S.op("dve", lambda h, lo=lo, hi=hi: h.tensor_tensor(out=ssum[:, 0:TOWN], in0=csu[:, hi:hi + TOWN], in1=csu[:, lo:lo + TOWN], op=ALU.subtract),
                     reads=["csu"], writes=[("tf", 1)])
                S.op("pool", lambda h, lo=lo, hi=hi: h.tensor_tensor(out=rc[:, 0:TOWN], in0=csm[:, hi:hi + TOWN], in1=csm[:, lo:lo + TOWN], op=ALU.subtract),
                     reads=["csm"], writes=[("tf", 2)])
                S.op("dve", lambda h: h.reciprocal(out=rc[:, 0:TOWN], in_=rc[:, 0:TOWN]), reads=[("tf", 2)], writes=[("tf", 2)])
                S.op("dve", lambda h: h.tensor_tensor(out=ssum[:, 0:TOWN], in0=ssum[:, 0:TOWN], in1=rc[:, 0:TOWN], op=ALU.mult),
                     reads=[("tf", 1), ("tf", 2)], writes=[("tf", 1)])
                S.op("dve", lambda h, ui=ui: h.tensor_tensor(out=tmpb[:], in0=ssum[:, 0:TOWN], in1=ub[ui][:, HALO:HALO + TOWN], op=ALU.subtract),
                     reads=[("tf", 1), ("ub", ui)], writes=["tmpb"])
                si = st["stg"] % 2
                st["stg"] += 1
                for (t0, n) in tiles_of(TOWN):
                    pi = C.next_ps()
                    S.op("pe", lambda h, g=g, pi=pi, t0=t0, n=n: h.matmul(C.ps[pi][:, 0:n], lhsT=pw_sb[:, g, :], rhs=tmpb[:, t0:t0 + n], start=True, stop=True),
                         reads=["tmpb", "poolw"], writes=[("ps", pi)])
                    S.op("act", lambda h, g=g, pi=pi, si=si, t0=t0, n=n: h.activation(out=stg[si][:, t0:t0 + n], in_=C.ps[pi][:, 0:n], func=AF.Copy,
                                                                                       scale=sp_sb[:, g:g + 1]),
                         reads=[("ps", pi), "consts"], writes=[("stg", si)])
                store(ypc[:, g, :], ("stg", si), stg[si][:])

        for c in range(4):
            uv, ug = proj_pair([4 + c, 8 + c])
            S.op("act", lambda h, ug=ug: h.activation(out=ub[ug][:], in_=ub[ug][:], func=AF.Sigmoid), reads=[("ub", ug)], writes=[("ub", ug)])
            S.op("dve", lambda h, c=c, uv=uv, ug=ug: h.tensor_tensor(out=zb[:, c, :], in0=ub[uv][:], in1=ub[ug][:], op=ALU.mult),
                 reads=[("ub", uv), ("ub", ug)], writes=[("zb", c)])
            for k in range(31):
                eng = "dve" if k % 2 == 0 else "pool"
                S.op(eng, lambda h, c=c, k=k: h.tensor_scalar(out=diag[:, k, :], in0=ident[:], scalar1=sp_sb[:, 20 + c * 31 + k:20 + c * 31 + k + 1],
                                                              scalar2=None, op0=ALU.mult),
                     reads=["ident", "consts"], writes=[("diag", k)])
            for (t0, n) in tiles_of(TOWN):
                pi = C.next_ps()

                def mmc(h, c=c, pi=pi, t0=t0, n=n):
                    for k in range(31):
                        ins = h.matmul(C.ps[pi][:, 0:n], lhsT=diag[:, k, :], rhs=zb[:, c, t0 + 1 + k:t0 + 1 + k + n], start=(k == 0), stop=(k == 30))
                    return ins

                S.op("pe", mmc, reads=[("zb", c)] + [("diag", k) for k in range(31)], writes=[("ps", pi)])
                S.op("act", lambda h, c=c, pi=pi, t0=t0, n=n: h.activation(out=z2[:, c, t0:t0 + n], in_=C.ps[pi][:, 0:n], func=AF.Identity,
                                                                          bias=sp_sb[:, 4 + c:5 + c]),
                     reads=[("ps", pi), "consts"], writes=[("z2", c, t0)])
        for (t0, n) in tiles_of(TOWN):
            pm = C.next_ps()
            pq = C.next_ps()
            for c in range(4):
                S.op("pe", lambda h, c=c, pm=pm, t0=t0, n=n: h.matmul(C.ps[pm][:, 0:n], lhsT=ones_f[:], rhs=z2[:, c, t0:t0 + n], start=(c == 0), stop=(c == 3)),
                     reads=[("z2", c, t0), "ones"], writes=[("ps", pm)])
            for c in range(4):
                S.op("act", lambda h, c=c, t0=t0, n=n: h.activation(out=tmpf[c][:, 0:n], in_=z2[:, c, t0:t0 + n], func=AF.Square),
                     reads=[("z2", c, t0)], writes=[("tf", c)])
                S.op("pe", lambda h, c=c, pq=pq, n=n: h.matmul(C.ps[pq][:, 0:n], lhsT=ones_f[:], rhs=tmpf[c][:, 0:n], start=(c == 0), stop=(c == 3)),
                     reads=[("tf", c), "ones"], writes=[("ps", pq)])
            mean, var = tmpf[4], tmpf[5]
            S.op("dve", lambda h, pm=pm, n=n: h.tensor_scalar(out=mean[:, 0:n], in0=C.ps[pm][:, 0:n], scalar1=1.0 / 512, scalar2=None, op0=ALU.mult),
                 reads=[("ps", pm)], writes=[("tf", 4)])
            S.op("dve", lambda h, pq=pq, n=n: h.tensor_scalar(out=var[:, 0:n], in0=C.ps[pq][:, 0:n], scalar1=1.0 / 512, scalar2=LN_EPS, op0=ALU.mult, op1=ALU.add),
                 reads=[("ps", pq)], writes=[("tf", 5)])
            S.op("dve", lambda h, n=n: h.tensor_tensor(out=tmpf[0][:, 0:n], in0=mean[:, 0:n], in1=mean[:, 0:n], op=ALU.mult),
                 reads=[("tf", 4)], writes=[("tf", 0)])
            S.op("dve", lambda h, n=n: h.tensor_tensor(out=var[:, 0:n], in0=var[:, 0:n], in1=tmpf[0][:, 0:n], op=ALU.subtract),
                 reads=[("tf", 5), ("tf", 0)], writes=[("tf", 5)])
            S.op("act", lambda h, n=n: h.activation(out=var[:, 0:n], in_=var[:, 0:n], func=AF.Sqrt), reads=[("tf", 5)], writes=[("tf", 5)])
            S.op("dve", lambda h, n=n: h.reciprocal(out=var[:, 0:n], in_=var[:, 0:n]), reads=[("tf", 5)], writes=[("tf", 5)])
            for c in range(4):
                t1 = tmpf[c % 2]
                S.op("dve", lambda h, c=c, t1=t1, t0=t0, n=n: h.tensor_tensor(out=t1[:, 0:n], in0=z2[:, c, t0:t0 + n], in1=mean[:, 0:n], op=ALU.subtract),
                     reads=[("z2", c, t0), ("tf", 4)], writes=[("tf", c % 2)])
                S.op("dve", lambda h, c=c, t1=t1, n=n: h.tensor_tensor(out=t1[:, 0:n], in0=t1[:, 0:n], in1=var[:, 0:n], op=ALU.mult),
                     reads=[("tf", c % 2), ("tf", 5)], writes=[("tf", c % 2)])
                S.op("act", lambda h, c=c, t1=t1, t0=t0, n=n: h.activation(out=z2[:, c, t0:t0 + n], in_=t1[:, 0:n], func=AF.Silu,
                                                                          scale=sp_sb[:, 8 + c:9 + c], bias=sp_sb[:, 12 + c:13 + c]),
                     reads=[("tf", c % 2), "consts"], writes=[("z2", c, t0)])
        for c in range(4):
            S.op("sp", lambda h, c=c: h.dma_start(out=ypc[:, 4 + c, :], in_=z2[:, c, :]),
                 reads=[("z2", c, t0) for (t0, n) in tiles_of(TOWN)], dma=True)

        for gp in range(4):
            uis = proj_pair([12 + 2 * gp, 13 + 2 * gp])
            for q in range(2):
                c = 2 * gp + q
                ui = uis[q]
                t1 = tmpf[0][:, 0:TOWN]
                si = st["stg"] % 2
                st["stg"] += 1
                uo = ub[ui][:, HALO:HALO + TOWN]
                S.op("pool", lambda h, uo=uo, t1=t1: h.tensor_tensor(out=t1, in0=uo, in1=uo, op=ALU.mult), reads=[("ub", ui)], writes=[("tf", 0)])
                S.op("dve", lambda h, t1=t1: h.tensor_scalar(out=t1, in0=t1, scalar1=0.044715, scalar2=1.0, op0=ALU.mult, op1=ALU.add),
                     reads=[("tf", 0)], writes=[("tf", 0)])
                S.op("dve", lambda h, uo=uo, t1=t1: h.tensor_tensor(out=t1, in0=t1, in1=uo, op=ALU.mult), reads=[("tf", 0), ("ub", ui)], writes=[("tf", 0)])
                S.op("act", lambda h, t1=t1: h.activation(out=t1, in_=t1, func=AF.Sigmoid, scale=1.5957691216057308), reads=[("tf", 0)], writes=[("tf", 0)])
                S.op("dve", lambda h, uo=uo, si=si, t1=t1: h.tensor_tensor(out=stg[si][:], in0=t1, in1=uo, op=ALU.mult),
                     reads=[("tf", 0), ("ub", ui)], writes=[("stg", si)])
                store(gout[:, c, :], ("stg", si), stg[si][:])

        S.op("dve", lambda h: h.memset(accr[:], 0.0), writes=["accr"])
        tmp5 = [(tmpf[i], ("tf", i)) for i in range(5)]
        hout = tmpf[5][:, 0:TOWN]
        for gp in range(4):
            uis = proj_pair([20 + 2 * gp, 21 + 2 * gp])
            for q in range(2):
                c = 2 * gp + q
                ui = uis[q]
                store(xrout[:, c, :], ("ub", ui), ub[ui][:])
                for d in range(2):
                    col = d * 8 + c
                    emit_lru_head(C, c, d, ub[ui], ("ub", ui), HALO, prm, tmp5, tmpb, 0.0, hout, ("tf", 5), accr[:, col, :])
                    last = TOWN - 1 if d == 0 else 0
                    S.op("pool", lambda h, col=col, last=last: h.tensor_copy(out=car_sb[:, 16 + col:17 + col], in_=hout[:, last:last + 1]),
                         reads=[("tf", 5)], writes=["car"])
        S.op("dve", lambda h: h.tensor_tensor(out=car_sb[:, 0:16], in0=accr[:, :, 0], in1=accr[:, :, 1], op=ALU.add),
             reads=["accr"], writes=["car"])
        S.op("sp", lambda h: h.dma_start(out=car, in_=car_sb[:]), reads=["car"], dma=True)
        _finish(nc, C, block, sems, rings)
    return nc


def build_B(final):
    nc = bass.Bass("TRN2", target_bir_lowering=False)

    def din(name, shape, dt=F32):
        return nc.dram_tensor(name, shape, dt, kind="ExternalInput").ap()

    def dout(name, shape, dt=F32):
        return nc.dram_tensor(name, shape, dt, kind="ExternalOutput").ap()

    xin = din("xin", [128, DC, TOWN])
    ypc = din("ypc", [128, 8, TOWN])
    gin = din("gin", [128, 8, TOWN])
    xrin = din("xrin", [128, 8, TA])
    carall = din("carall", [128, NCORES, 32])
    onehot = din("onehot", [128, NCORES])
    norms = din("norms", [128, 2 * DC])
    w_out = din("w_out", [D, D])
    wg = din("wg", [D, DFF]); wu = din("wu", [D, DFF]); wd = din("wd", [DFF, D])
    lru_d = _lru_dram(din)
    xout = dout("xout", [128, DC, TOWN])

    with ExitStack() as es:
        C = Ctx(nc, es)
        S = C.S
        sems = {e: es.enter_context(nc.semaphore("s_" + e)) for e in ENGS}
        rings = {e: [es.enter_context(nc.semaphore("r_%s%d" % (e, i))) for i in range(NRING)] for e in ("sp", "pool")}
        block = es.enter_context(nc.Block())

        x_sb = C.alloc("x", [DC, TOWN], F32)
        g_sb = C.alloc("g", [2 * DC], F32)
        ones_f = C.alloc("ones_f", [128], F32)
        S.op("dve", lambda h: h.memset(ones_f[:], 1.0), writes=["ones"])
        S.op("sp", lambda h: h.dma_start(out=g_sb[:], in_=norms), writes=["consts"], dma=True)
        m_y = C.mark()
        y_sb = C.alloc("y", [DC, TOWN], BF16)
        for c in range(0, 8, 4):
            S.op("pool", lambda h, c=c: h.dma_start(out=y_sb[:, c:c + 4, :], in_=ypc[:, c:c + 4, :]),
                 writes=[("y", cc) for cc in range(c, c + 4)], dma=True)
        m_l = C.mark()
        prm = load_lru_params(C, lru_d)
        ca = C.alloc("carall", [NCORES, 32], F32)
        oh = C.alloc("onehot", [NCORES], F32)
        S.op("sp", lambda h: h.dma_start(out=ca[:], in_=carall), writes=["carall"], dma=True)
        S.op("sp", lambda h: h.dma_start(out=oh[:], in_=onehot), writes=["consts"], dma=True)
        at = C.alloc("atot", [NCORES, 16], F32)
        for j in range(NCORES):
            S.op("dve", lambda h, j=j: h.tensor_tensor(out=at[:, j, :], in0=ca[:, j, 0:16], in1=prm["c1"][:], op=ALU.mult),
                 reads=["carall", "c1"], writes=[("at", j)])
            S.op("act", lambda h, j=j: h.activation(out=at[:, j, :], in_=at[:, j, :], func=AF.Exp), reads=[("at", j)], writes=[("at", j)])
        cin = C.alloc("cin", [NCORES, 16], F32)
        S.op("dve", lambda h: h.memset(cin[:], 0.0), writes=["cin"])
        for j in range(1, NCORES):
            S.op("dve", lambda h, j=j: h.tensor_tensor(out=cin[:, j, 0:8], in0=at[:, j - 1, 0:8], in1=cin[:, j - 1, 0:8], op=ALU.mult),
                 reads=[("at", j - 1), "cin"], writes=["cin"])
            S.op("dve", lambda h, j=j: h.tensor_tensor(out=cin[:, j, 0:8], in0=cin[:, j, 0:8], in1=ca[:, j - 1, 16:24], op=ALU.add),
                 reads=["cin", "carall"], writes=["cin"])
        for j in range(NCORES - 2, -1, -1):
            S.op("dve", lambda h, j=j: h.tensor_tensor(out=cin[:, j, 8:16], in0=at[:, j + 1, 8:16], in1=cin[:, j + 1, 8:16], op=ALU.mult),
                 reads=[("at", j + 1), "cin"], writes=["cin"])
            S.op("dve", lambda h, j=j: h.tensor_tensor(out=cin[:, j, 8:16], in0=cin[:, j, 8:16], in1=ca[:, j + 1, 24:32], op=ALU.add),
                 reads=["cin", "carall"], writes=["cin"])
        mine = C.alloc("mine", [16], F32)
        S.op("dve", lambda h: h.tensor_scalar(out=mine[:], in0=cin[:, 0, :], scalar1=oh[:, 0:1], scalar2=None, op0=ALU.mult),
             reads=["cin", "consts"], writes=["carry"])
        for j in range(1, NCORES):
            S.op("dve", lambda h, j=j: h.scalar_tensor_tensor(out=mine[:], in0=cin[:, j, :], scalar=oh[:, j:j + 1], in1=mine[:],
                                                               op0=ALU.mult, op1=ALU.add),
                 reads=["cin", "consts", "carry"], writes=["carry"])

        xrb = [C.alloc("xrb%d" % i, [TA], F32) for i in range(2)]
        gb = [C.alloc("gb%d" % i, [TOWN], F32) for i in range(2)]
        tmpf = [C.alloc("tf%d" % i, [TA], F32) for i in range(5)]
        tmpb = C.alloc("tb", [TOWN], BF16)
        hf = C.alloc("hf", [TOWN], F32)
        hb = C.alloc("hb", [TOWN], F32)
        tmp5 = [(tmpf[i], ("tf", i)) for i in range(5)]
        for c in range(8):
            bi = c % 2
            S.op("sp", lambda h, c=c, bi=bi: h.dma_start(out=xrb[bi][:], in_=xrin[:, c, :]), writes=[("xrb", bi)], dma=True)
            S.op("sp", lambda h, c=c, bi=bi: h.dma_start(out=gb[bi][:], in_=gin[:, c, :]), writes=[("gb", bi)], dma=True)
            if c == 1:
                emit_load_x(C, x_sb, xin, TOWN)
            emit_lru_head(C, c, 0, xrb[bi], ("xrb", bi), HALO, prm, tmp5, tmpb, mine[:, c:c + 1], hf[:], "hf", None)
            emit_lru_head(C, c, 1, xrb[bi], ("xrb", bi), HALO, prm, tmp5, tmpb, mine[:, 8 + c:9 + c], hb[:], "hb", None)
            S.op("dve", lambda h: h.tensor_tensor(out=hf[:], in0=hf[:], in1=hb[:], op=ALU.add), reads=["hf", "hb"], writes=["hf"])
            S.op("dve", lambda h, c=c, bi=bi: h.tensor_tensor(out=y_sb[:, 8 + c, :], in0=hf[:], in1=gb[bi][:], op=ALU.mult),
                 reads=["hf", ("gb", bi)], writes=[("y", 8 + c)])
        C.release(m_l)

        NWR = 3
        wob = [C.alloc("wob%d" % i, [DC, 256], BF16) for i in range(NWR)]
        w_ov = w_out.rearrange("(kc p) n -> p kc n", p=128)
        for n2 in range(DC // 2):
            wi = n2 % NWR
            S.op("pool", lambda h, n2=n2, wi=wi: h.dma_start(out=wob[wi][:], in_=w_ov[:, :, n2 * 256:(n2 + 1) * 256]),
                 writes=[("wob", wi)], dma=True)
            for nn in range(2):
                nchunk = 2 * n2 + nn
                for (t0, n) in tiles_of(TOWN):
                    po = C.next_ps()

                    def mmo(h, wi=wi, nn=nn, po=po, t0=t0, n=n):
                        for kc in range(DC):
                            ins = h.matmul(C.ps[po][:, 0:n], lhsT=wob[wi][:, kc, nn * 128:(nn + 1) * 128], rhs=y_sb[:, kc, t0:t0 + n],
                                           start=(kc == 0), stop=(kc == DC - 1))
                        return ins

                    S.op("pe", mmo, reads=[("wob", wi)] + [("y", kc) for kc in range(DC)], writes=[("ps", po)])
                    S.op("dve", lambda h, po=po, nchunk=nchunk, t0=t0, n=n: h.tensor_tensor(
                        out=x_sb[:, nchunk, t0:t0 + n], in0=C.ps[po][:, 0:n], in1=x_sb[:, nchunk, t0:t0 + n], op=ALU.add),
                        reads=[("ps", po), ("x", nchunk)], writes=[("x", nchunk)])
        C.release(m_y)

        h_sb = C.alloc("h", [DC, TOWN], BF16)
        emit_rmsnorm(C, x_sb, g_sb, 0, h_sb, TOWN, ones_f)
        emit_ffn(C, x_sb, h_sb, TOWN, wg, wu, wd)
        if final:
            C.release(C.mark())
            C.off = m_y
            o_sb = C.alloc("o", [DC, TOWN], F32)
            emit_rmsnorm(C, x_sb, g_sb, DC, None, TOWN, ones_f, hkey="o", out_f32=o_sb)
            for c in range(0, DC, 4):
                S.op("sp", lambda h, c=c: h.dma_start(out=xout[:, c:c + 4, :], in_=o_sb[:, c:c + 4, :]),
                     reads=[("o", cc) for cc in range(c, c + 4)], dma=True)
        else:
            for c in range(0, DC, 4):
                S.op("sp", lambda h, c=c: h.dma_start(out=xout[:, c:c + 4, :], in_=x_sb[:, c:c + 4, :]),
                     reads=[("x", cc) for cc in range(c, c + 4)], dma=True)
        _finish(nc, C, block, sems, rings)
    return nc


_CACHE = {}


def _prog(name):
    if name not in _CACHE:
        _CACHE[name] = build_A() if name == "A" else build_B(name == "Bf")
    return _CACHE[name]


def _fm(v):
    v = np.asarray(v, dtype=np.float32)
    return np.ascontiguousarray(v.reshape(-1, 128).T)


def _run(nc, in_maps):
    res = run_bass_kernel_spmd(nc, in_maps, core_ids=list(range(NCORES)))
    return res.results


def kernel(x, norm_ffn1, ffn1_w_gate, ffn1_w_up, ffn1_w_down, norm_mix, w_in,
           pool_w, pool_scale, conv_dw_w, conv_dw_b, conv_ln_g, conv_ln_b,
           lru_conv_w, lru_conv_b, lru_w_a, lru_b_a, lru_w_x, lru_b_x, lru_lambda,
           w_out, norm_ffn2, ffn2_w_gate, ffn2_w_up, ffn2_w_down, norm_final, _nlayers=DEPTH, _debug=None):
    f32 = np.float32
    A = lambda a: np.ascontiguousarray(np.asarray(a, dtype=f32))
    x = A(x)
    xfm = np.ascontiguousarray(x.reshape(SEQ, DC, 128).transpose(2, 1, 0))
    ident = np.eye(128, dtype=f32)
    vfull = np.zeros((SEQ + 2 * HALO,), f32)
    vfull[HALO:HALO + SEQ] = 1.0
    onehots = [np.ascontiguousarray(np.broadcast_to(np.eye(NCORES, dtype=f32)[c][None, :], (128, NCORES))) for c in range(NCORES)]
    for l in range(_nlayers):
        lp = dict(
            lru_conv_w=np.ascontiguousarray(A(lru_conv_w[l]).reshape(2, 4, 8, 128).transpose(3, 0, 2, 1)),
            lru_conv_b=_fm(lru_conv_b[l]), lru_b_a=_fm(lru_b_a[l]), lru_b_x=_fm(lru_b_x[l]), lru_lambda=_fm(lru_lambda[l]),
            lru_w_a=np.ascontiguousarray(A(lru_w_a[l]).transpose(2, 0, 1, 3)),
            lru_w_x=np.ascontiguousarray(A(lru_w_x[l]).transpose(2, 0, 1, 3)),
        )
        smallp = np.zeros((128, 20 + 4 * 31), f32)
        smallp[:, 0:4] = _fm(pool_scale[l])
        smallp[:, 4:8] = _fm(conv_dw_b[l])
        smallp[:, 8:12] = _fm(conv_ln_g[l])
        smallp[:, 12:16] = _fm(conv_ln_b[l])
        smallp[:, 20:] = A(conv_dw_w[l]).reshape(31, 4, 128).transpose(2, 1, 0).reshape(128, 124)
        commonA = dict(norms=np.concatenate([_fm(norm_ffn1[l]), _fm(norm_mix[l])], axis=1),
                       wg=A(ffn1_w_gate[l]), wu=A(ffn1_w_up[l]), wd=A(ffn1_w_down[l]), w_in=A(w_in[l]),
                       pool_w=np.ascontiguousarray(A(pool_w[l]).transpose(1, 0, 2)), smallp=smallp, ident=ident, **lp)
        xpad = np.zeros((128, DC, SEQ + 2 * HALO), f32)
        xpad[:, :, HALO:HALO + SEQ] = xfm
        in_maps = []
        for c in range(NCORES):
            s = c * TOWN
            in_maps.append(dict(xin=np.ascontiguousarray(xpad[:, :, s:s + TA]),
                                vmask=np.ascontiguousarray(np.broadcast_to(vfull[s:s + TA][None, :], (128, TA))), **commonA))
        ra = _run(_prog("A"), in_maps)
        if _debug is not None:
            _debug["A%d" % l] = ra
        carall = np.ascontiguousarray(np.stack([ra[c]["car"] for c in range(NCORES)], axis=1))
        last = (l == DEPTH - 1)
        commonB = dict(norms=np.concatenate([_fm(norm_ffn2[l]), _fm(norm_final)], axis=1), w_out=A(w_out[l]),
                       wg=A(ffn2_w_gate[l]), wu=A(ffn2_w_up[l]), wd=A(ffn2_w_down[l]), carall=carall, **lp)
        in_maps = []
        for c in range(NCORES):
            in_maps.append(dict(xin=ra[c]["xout"], ypc=ra[c]["ypc"], gin=ra[c]["gout"], xrin=ra[c]["xrout"],
                                onehot=onehots[c], **commonB))
        rb = _run(_prog("Bf" if last else "B"), in_maps)
        if _debug is not None:
            _debug["B%d" % l] = rb
        xfm = np.concatenate([rb[c]["xout"] for c in range(NCORES)], axis=2)
    out = np.ascontiguousarray(xfm.transpose(2, 1, 0)).reshape(1, SEQ, D)
    return out.astype(np.float32)
```

```python
import numpy as np
from contextlib import ExitStack
import concourse.bass as bass
import concourse.mybir as mybir
from concourse.bass_utils import run_bass_kernel_spmd

F32 = mybir.dt.float32
BF16 = mybir.dt.bfloat16
AF = mybir.ActivationFunctionType
ALU = mybir.AluOpType

ENGS = ("pe", "act", "dve", "pool", "sp")
NRING = 6
SAME_ENG_DIST = 4

NCORES = 8
D = 2048
DC = 16
DFF = 5632
FC = 44
DIN = 3584
SEQ = 8192
TOWN = 1024
HALO = 16
TA = TOWN + 2 * HALO
DEPTH = 4
RMS_EPS = 1e-6
LN_EPS = 1e-5


class _Op:
    __slots__ = ("id", "eng", "fn", "deps", "dma", "seq", "pos", "ring", "rval", "ndep")


class Sched:
    def __init__(self):
        self.ops = []
        self.by_eng = {e: [] for e in ENGS}
        self.last_w = {}
        self.readers = {}
        self.pending_bar = {}
        self.dma_since_bar = []

    def op(self, eng, fn, reads=(), writes=(), dma=False):
        o = _Op()
        o.id = len(self.ops)
        o.eng = eng
        o.fn = fn
        o.dma = dma
        o.seq = None
        o.ndep = 0
        deps = set()
        for r in reads:
            w = self.last_w.get(r)
            if w is not None:
                deps.add(w)
        for k in writes:
            w = self.last_w.get(k)
            if w is not None:
                deps.add(w)
            rd = self.readers.get(k)
            if rd:
                deps.update(rd.values())
        b = self.pending_bar.pop(eng, None)
        if b:
            deps.update(b)
        o.deps = deps
        for r in reads:
            d = self.readers.setdefault(r, {})
            d[("dma", o.id) if dma else eng] = o.id
        for k in writes:
            self.last_w[k] = o.id
            self.readers[k] = {}
        o.pos = len(self.by_eng[eng])
        self.ops.append(o)
        self.by_eng[eng].append(o)
        if dma:
            self.dma_since_bar.append(o.id)
        return o

    def barrier(self):
        b = set(self.dma_since_bar)
        for e in ENGS:
            if self.by_eng[e]:
                b.add(self.by_eng[e][-1].id)
        for e in ENGS:
            self.pending_bar.setdefault(e, set()).update(b)
        self.dma_since_bar = []

    def prepare(self):
        ops = self.ops
        for o in ops:
            for d in o.deps:
                p = ops[d]
                if p.dma:
                    continue
                if p.eng == o.eng and (p.eng == "pe" or o.pos - p.pos >= SAME_ENG_DIST):
                    continue
                p.ndep += 1
        for e in ENGS:
            s = 0
            k = 0
            for o in self.by_eng[e]:
                if o.dma:
                    o.ring = k % NRING
                    o.rval = 16 * (k // NRING + 1)
                    k += 1
                elif o.ndep:
                    s += 1
                    o.seq = s

    def run(self, e, h, sems, rings):
        ops = self.ops
        waited = {}
        for o in self.by_eng[e]:
            need = {}
            for d in o.deps:
                p = ops[d]
                if p.dma:
                    sem = rings[p.eng][p.ring]
                    val = p.rval
                else:
                    if p.seq is None:
                        continue
                    if p.eng == e and (e == "pe" or o.pos - p.pos >= SAME_ENG_DIST):
                        continue
                    sem = sems[p.eng]
                    val = p.seq
                k = id(sem)
                if k not in need or need[k][1] < val:
                    need[k] = (sem, val)
            if o.dma and o.rval > 16:
                sem = rings[e][o.ring]
                k = id(sem)
                v = o.rval - 16
                if k not in need or need[k][1] < v:
                    need[k] = (sem, v)
            for k, (sem, val) in need.items():
                if waited.get(k, 0) >= val:
                    continue
                waited[k] = val
                h.wait_ge(sem, val)
            ins = o.fn(h)
            if o.dma:
                ins.then_inc(rings[e][o.ring], 16)
            elif o.seq is not None:
                ins.then_inc(sems[e], 1)
        last = {}
        for o in self.by_eng[e]:
            if o.dma:
                last[o.ring] = o.rval
        for r, v in last.items():
            h.wait_ge(rings[e][r], v)


class Ctx:
    def __init__(self, nc, es, arena_bytes=200 * 1024):
        self.nc = nc
        self.S = Sched()
        self.arena = es.enter_context(nc.sbuf_tensor("arena", [128, arena_bytes // 2], BF16))
        self.arena_bytes = arena_bytes
        self.off = 0
        self.ps = [es.enter_context(nc.psum_tensor("ps%d" % i, [128, 512], F32)) for i in range(8)]
        self.psi = 0
        self.uid = 0

    def alloc(self, name, free_shape, dt):
        n = int(np.prod(free_shape))
        esz = 2 if dt == BF16 else 4
        nbytes = (n * esz + 63) // 64 * 64
        assert self.off + nbytes <= self.arena_bytes, (name, self.off, nbytes, self.arena_bytes)
        v = self.arena[:, self.off // 2:(self.off + n * esz) // 2]
        if dt != BF16:
            v = v.bitcast(dt)
        if len(free_shape) == 2:
            v = v.rearrange("p (a b) -> p a b", a=free_shape[0])
        elif len(free_shape) == 3:
            v = v.rearrange("p (a b c) -> p a b c", a=free_shape[0], b=free_shape[1])
        self.off += nbytes
        return v

    def mark(self):
        return self.off

    def release(self, m):
        self.S.barrier()
        self.off = m

    def next_ps(self):
        i = self.psi
        self.psi = (self.psi + 1) % 8
        return i

    def key(self, base):
        self.uid += 1
        return "%s#%d" % (base, self.uid)


def tiles_of(T):
    out = []
    t = 0
    while t < T:
        n = min(512, T - t)
        out.append((t, n))
        t += n
    return out


def emit_load_x(C, x_sb, x_dram, T):
    S = C.S
    for c in range(0, DC, 4):
        S.op("sp", lambda h, c=c: h.dma_start(out=x_sb[:, c:c + 4, :], in_=x_dram[:, c:c + 4, :]),
             writes=[("x", cc) for cc in range(c, c + 4)], dma=True)


def emit_rmsnorm(C, x_sb, g_sb, gcol0, h_sb, T, ones_f, hkey="h", out_f32=None):
    S = C.S
    m = C.mark()
    NSQ = 4
    sq = [C.alloc("sq%d" % i, [512], BF16) for i in range(NSQ)]
    ones_b = C.alloc("ones_b", [128], BF16)
    S.op("dve", lambda h: h.memset(ones_b[:], 1.0), writes=["ones_b"])
    rstd = C.alloc("rstd", [T], F32)
    epsc = C.alloc("epsc", [1], F32)
    S.op("dve", lambda h: h.memset(epsc[:], RMS_EPS), writes=["epsc"])
    for (t0, n) in tiles_of(T):
        pi = C.next_ps()
        ps = C.ps[pi]
        for c in range(DC):
            i = c % NSQ
            eng = "act" if c % 2 == 0 else "dve"
            if eng == "act":
                S.op("act", lambda h, c=c, i=i, t0=t0, n=n: h.activation(out=sq[i][:, 0:n], in_=x_sb[:, c, t0:t0 + n], func=AF.Square),
                     reads=[("x", c)], writes=[("sq", i)])
            else:
                S.op("dve", lambda h, c=c, i=i, t0=t0, n=n: h.tensor_tensor(out=sq[i][:, 0:n], in0=x_sb[:, c, t0:t0 + n],
                                                                  in1=x_sb[:, c, t0:t0 + n], op=ALU.mult),
                     reads=[("x", c)], writes=[("sq", i)])
            S.op("pe", lambda h, c=c, i=i, ps=ps, n=n: h.matmul(ps[:, 0:n], lhsT=ones_b[:], rhs=sq[i][:, 0:n],
                                                     start=(c == 0), stop=(c == DC - 1)),
                 reads=[("sq", i), "ones_b"], writes=[("ps", pi)])
        S.op("act", lambda h, t0=t0, n=n, ps=ps: h.activation(out=rstd[:, t0:t0 + n], in_=ps[:, 0:n], func=AF.Sqrt, scale=1.0 / D,
                                                               bias=epsc[:, 0:1]),
             reads=[("ps", pi), "epsc"], writes=[("rstd", t0)])
        S.op("dve", lambda h, t0=t0, n=n: h.reciprocal(out=rstd[:, t0:t0 + n], in_=rstd[:, t0:t0 + n]),
             reads=[("rstd", t0)], writes=[("rstd", t0)])
        for c in range(DC):
            eng = "dve"
            dst = (h_sb[:, c, t0:t0 + n] if out_f32 is None else out_f32[:, c, t0:t0 + n])
            S.op(eng, lambda h, c=c, dst=dst, t0=t0, n=n: h.scalar_tensor_tensor(out=dst, in0=x_sb[:, c, t0:t0 + n],
                                                                      scalar=g_sb[:, gcol0 + c:gcol0 + c + 1],
                                                                      in1=rstd[:, t0:t0 + n], op0=ALU.mult, op1=ALU.mult),
                 reads=[("x", c), ("rstd", t0), "consts"], writes=[(hkey, c)])
    C.release(m)


def emit_ffn(C, x_sb, h_sb, T, wg, wu, wd):
    S = C.S
    TT = tiles_of(T)
    NPH = 4
    JP = FC // NPH
    m = C.mark()
    hid = C.alloc("hid", [JP, T], BF16)
    NWR = 3
    wgb = [C.alloc("wgb%d" % i, [DC, 128], BF16) for i in range(NWR)]
    wub = [C.alloc("wub%d" % i, [DC, 128], BF16) for i in range(NWR)]
    wdb = [C.alloc("wdb%d" % i, [JP, 256], BF16) for i in range(2)]
    sg = [C.alloc("sg%d" % i, [512], F32) for i in range(3)]
    wgv = wg.rearrange("(kc p) n -> p kc n", p=128)
    wuv = wu.rearrange("(kc p) n -> p kc n", p=128)
    wdv = wd.rearrange("(j p) n -> p j n", p=128)
    wcnt = 0
    scnt = 0
    dcnt = 0
    for ph in range(NPH):
        for jj in range(JP):
            j = ph * JP + jj
            wi = wcnt % NWR
            wcnt += 1
            S.op("pool", lambda h, j=j, wi=wi: h.dma_start(out=wgb[wi][:], in_=wgv[:, :, j * 128:(j + 1) * 128]),
                 writes=[("wgb", wi)], dma=True)
            S.op("pool", lambda h, j=j, wi=wi: h.dma_start(out=wub[wi][:], in_=wuv[:, :, j * 128:(j + 1) * 128]),
                 writes=[("wub", wi)], dma=True)
            for (t0, n) in TT:
                pg = C.next_ps()
                pu = C.next_ps()

                def mm_g(h, wi=wi, pg=pg, t0=t0, n=n):
                    for kc in range(DC):
                        ins = h.matmul(C.ps[pg][:, 0:n], lhsT=wgb[wi][:, kc, :], rhs=h_sb[:, kc, t0:t0 + n],
                                       start=(kc == 0), stop=(kc == DC - 1))
                    return ins

                def mm_u(h, wi=wi, pu=pu, t0=t0, n=n):
                    for kc in range(DC):
                        ins = h.matmul(C.ps[pu][:, 0:n], lhsT=wub[wi][:, kc, :], rhs=h_sb[:, kc, t0:t0 + n],
                                       start=(kc == 0), stop=(kc == DC - 1))
                    return ins

                hreads = [("h", c) for c in range(DC)]
                S.op("pe", mm_g, reads=[("wgb", wi)] + hreads, writes=[("ps", pg)])
                S.op("pe", mm_u, reads=[("wub", wi)] + hreads, writes=[("ps", pu)])
                si = scnt % 3
                scnt += 1
                S.op("act", lambda h, si=si, pg=pg, n=n: h.activation(out=sg[si][:, 0:n], in_=C.ps[pg][:, 0:n], func=AF.Silu),
                     reads=[("ps", pg)], writes=[("sg", si)])
                S.op("dve", lambda h, si=si, pu=pu, jj=jj, t0=t0, n=n: h.tensor_tensor(
                    out=hid[:, jj, t0:t0 + n], in0=sg[si][:, 0:n], in1=C.ps[pu][:, 0:n], op=ALU.mult),
                    reads=[("sg", si), ("ps", pu)], writes=[("hid", jj, t0)])
        for n2 in range(DC // 2):
            di = dcnt % 2
            dcnt += 1
            S.op("pool", lambda h, di=di, n2=n2, ph=ph: h.dma_start(
                out=wdb[di][:], in_=wdv[:, ph * JP:(ph + 1) * JP, n2 * 256:(n2 + 1) * 256]),
                writes=[("wdb", di)], dma=True)
            for nn in range(2):
                nchunk = n2 * 2 + nn
                for (t0, n) in TT:
                    po = C.next_ps()

                    def mm_d(h, di=di, nn=nn, po=po, t0=t0, n=n):
                        for jj in range(JP):
                            ins = h.matmul(C.ps[po][:, 0:n], lhsT=wdb[di][:, jj, nn * 128:(nn + 1) * 128],
                                           rhs=hid[:, jj, t0:t0 + n], start=(jj == 0), stop=(jj == JP - 1))
                        return ins

                    S.op("pe", mm_d, reads=[("wdb", di)] + [("hid", jj, t0) for jj in range(JP)], writes=[("ps", po)])
                    S.op("dve", lambda h, po=po, nchunk=nchunk, t0=t0, n=n: h.scalar_tensor_tensor(
                        out=x_sb[:, nchunk, t0:t0 + n], in0=C.ps[po][:, 0:n], scalar=0.5, in1=x_sb[:, nchunk, t0:t0 + n],
                        op0=ALU.mult, op1=ALU.add),
                        reads=[("ps", po), ("x", nchunk)], writes=[("x", nchunk)])
    C.release(m)


def emit_softplus_c(C, lam_sb, c1, c2, ncols):
    S = C.S
    m = C.mark()
    e = C.alloc("sp_e", [ncols], F32)
    s = C.alloc("sp_s", [ncols], F32)
    s2 = C.alloc("sp_s2", [ncols], F32)
    acc = C.alloc("sp_acc", [ncols], F32)
    S.op("act", lambda h: h.activation(out=e[:], in_=lam_sb, func=AF.Exp, scale=-1.0), reads=["consts"], writes=["sp_e"])
    S.op("dve", lambda h: h.tensor_scalar(out=s[:], in0=e[:], scalar1=2.0, scalar2=None, op0=ALU.add),
         reads=["sp_e"], writes=["sp_s"])
    S.op("dve", lambda h: h.reciprocal(out=s[:], in_=s[:]), reads=["sp_s"], writes=["sp_s"])
    S.op("dve", lambda h: h.tensor_tensor(out=s[:], in0=e[:], in1=s[:], op=ALU.mult), reads=["sp_e", "sp_s"], writes=["sp_s"])
    S.op("dve", lambda h: h.tensor_tensor(out=s2[:], in0=s[:], in1=s[:], op=ALU.mult), reads=["sp_s"], writes=["sp_s2"])
    NT = 9
    S.op("dve", lambda h: h.memset(acc[:], 1.0 / (2 * NT + 1)), writes=["sp_acc"])
    for k in range(NT - 1, -1, -1):
        S.op("dve", lambda h, k=k: h.tensor_tensor(out=acc[:], in0=acc[:], in1=s2[:], op=ALU.mult),
             reads=["sp_acc", "sp_s2"], writes=["sp_acc"])
        S.op("dve", lambda h, k=k: h.tensor_scalar(out=acc[:], in0=acc[:], scalar1=1.0 / (2 * k + 1), scalar2=None, op0=ALU.add),
             reads=["sp_acc"], writes=["sp_acc"])
    S.op("dve", lambda h: h.scalar_tensor_tensor(out=c1, in0=acc[:], scalar=-16.0, in1=s[:], op0=ALU.mult, op1=ALU.mult),
         reads=["sp_acc", "sp_s"], writes=["c1"])
    S.op("dve", lambda h: h.tensor_scalar(out=c2, in0=c1, scalar1=2.0, scalar2=None, op0=ALU.mult),
         reads=["c1"], writes=["c2"])
    C.release(m)


def emit_lru_head(C, c, d, xr, xr_key, XO, prm, tmp, tmpb, init_ap, out_h, out_key, acc_r):
    S = C.S
    (xc, kxc), (r, kr), (ig, kig), (a, ka), (bt, kbt) = tmp
    col = d * 8 + c
    cw = prm["lcw"]

    def xs(k):
        off = (k - 3) if d == 0 else (3 - k)
        return xr[:, XO + off:XO + off + TOWN]

    S.op("dve", lambda h: h.tensor_scalar(out=xc[:, 0:TOWN], in0=xs(0), scalar1=cw[:, d, c, 0:1], scalar2=prm["lcb"][:, col:col + 1],
                                           op0=ALU.mult, op1=ALU.add),
         reads=[xr_key, "consts"], writes=[kxc])
    for k in range(1, 4):
        S.op("dve", lambda h, k=k: h.scalar_tensor_tensor(out=xc[:, 0:TOWN], in0=xs(k), scalar=cw[:, d, c, k:k + 1], in1=xc[:, 0:TOWN],
                                                           op0=ALU.mult, op1=ALU.add),
             reads=[xr_key, kxc, "consts"], writes=[kxc])
    S.op("pool", lambda h: h.tensor_copy(out=tmpb[:], in_=xc[:, 0:TOWN]), reads=[kxc], writes=["tmpb"])
    for ti, (t0, n) in enumerate(tiles_of(TOWN)):
        pa = C.next_ps()
        px = C.next_ps()
        S.op("pe", lambda h, pa=pa, t0=t0, n=n: h.matmul(C.ps[pa][:, 0:n], lhsT=prm["wa"][:, d, c, :], rhs=tmpb[:, t0:t0 + n],
                                                          start=True, stop=True),
             reads=["tmpb", "lruw"], writes=[("ps", pa)])
        S.op("pe", lambda h, px=px, t0=t0, n=n: h.matmul(C.ps[px][:, 0:n], lhsT=prm["wx"][:, d, c, :], rhs=tmpb[:, t0:t0 + n],
                                                          start=True, stop=True),
             reads=["tmpb", "lruw"], writes=[("ps", px)])
        if acc_r is not None:
            S.op("act", lambda h, pa=pa, t0=t0, n=n, ti=ti: h.activation(out=r[:, t0:t0 + n], in_=C.ps[pa][:, 0:n], func=AF.Sigmoid,
                                                                          bias=prm["ba"][:, col:col + 1], accum_out=acc_r[:, ti:ti + 1]),
                 reads=[("ps", pa), "consts"], writes=[kr, "accr"])
        else:
            S.op("act", lambda h, pa=pa, t0=t0, n=n: h.activation(out=r[:, t0:t0 + n], in_=C.ps[pa][:, 0:n], func=AF.Sigmoid,
                                                                   bias=prm["ba"][:, col:col + 1]),
                 reads=[("ps", pa), "consts"], writes=[kr])
        S.op("act", lambda h, px=px, t0=t0, n=n: h.activation(out=ig[:, t0:t0 + n], in_=C.ps[px][:, 0:n], func=AF.Sigmoid,
                                                               bias=prm["bx"][:, col:col + 1]),
             reads=[("ps", px), "consts"], writes=[kig])
    S.op("act", lambda h: h.activation(out=a[:, 0:TOWN], in_=r[:, 0:TOWN], func=AF.Exp, scale=prm["c1"][:, col:col + 1]),
         reads=[kr, "c1"], writes=[ka])
    S.op("act", lambda h: h.activation(out=bt[:, 0:TOWN], in_=r[:, 0:TOWN], func=AF.Exp, scale=prm["c2"][:, col:col + 1]),
         reads=[kr, "c2"], writes=[kbt])
    S.op("act", lambda h: h.activation(out=bt[:, 0:TOWN], in_=bt[:, 0:TOWN], func=AF.Sqrt, scale=-1.0, bias=prm["one"][:, 0:1]),
         reads=[kbt, "consts"], writes=[kbt])
    S.op("dve", lambda h: h.tensor_tensor(out=ig[:, 0:TOWN], in0=ig[:, 0:TOWN], in1=xc[:, 0:TOWN], op=ALU.mult), reads=[kig, kxc], writes=[kig])
    S.op("dve", lambda h: h.tensor_tensor(out=bt[:, 0:TOWN], in0=bt[:, 0:TOWN], in1=ig[:, 0:TOWN], op=ALU.mult), reads=[kbt, kig], writes=[kbt])
    if d == 0:
        S.op("dve", lambda h: h.tensor_tensor_scan(out=out_h, data0=a[:, 0:TOWN], data1=bt[:, 0:TOWN], initial=init_ap,
                                                    op0=ALU.mult, op1=ALU.add),
             reads=[ka, kbt, "carry"], writes=[out_key])
    else:
        S.op("dve", lambda h: h.tensor_tensor_scan(out=out_h[:, ::-1], data0=a[:, TOWN - 1::-1], data1=bt[:, TOWN - 1::-1], initial=init_ap,
                                                    op0=ALU.mult, op1=ALU.add),
             reads=[ka, kbt, "carry"], writes=[out_key])


def load_lru_params(C, prm_d):
    S = C.S
    prm = {}
    prm["lcw"] = C.alloc("lcw", [2, 8, 4], F32)
    for nm in ("lcb", "ba", "bx", "lam", "c1", "c2"):
        prm[nm] = C.alloc(nm, [16], F32)
    prm["one"] = C.alloc("one", [1], F32)
    prm["wa"] = C.alloc("wa", [2, 8, 128], BF16)
    prm["wx"] = C.alloc("wx", [2, 8, 128], BF16)
    S.op("dve", lambda h: h.memset(prm["one"][:], 1.0), writes=["one"])
    S.op("sp", lambda h: h.dma_start(out=prm["lcw"][:], in_=prm_d["lru_conv_w"]), writes=["consts"], dma=True)
    for nm, key in (("lcb", "lru_conv_b"), ("ba", "lru_b_a"), ("bx", "lru_b_x"), ("lam", "lru_lambda")):
        S.op("sp", lambda h, nm=nm, key=key: h.dma_start(out=prm[nm][:], in_=prm_d[key]), writes=["consts"], dma=True)
    S.op("pool", lambda h: h.dma_start(out=prm["wa"][:], in_=prm_d["lru_w_a"]), writes=["lruw"], dma=True)
    S.op("pool", lambda h: h.dma_start(out=prm["wx"][:], in_=prm_d["lru_w_x"]), writes=["lruw"], dma=True)
    emit_softplus_c(C, prm["lam"][:], prm["c1"][:], prm["c2"][:], 16)
    return prm


def _finish(nc, C, block, sems, rings):
    S = C.S
    S.prepare()

    @block.tensor
    def _(h):
        S.run("pe", h, sems, rings)

    @block.scalar
    def _(h):
        S.run("act", h, sems, rings)

    @block.vector
    def _(h):
        S.run("dve", h, sems, rings)

    @block.gpsimd
    def _(h):
        S.run("pool", h, sems, rings)

    @block.sync
    def _(h):
        S.run("sp", h, sems, rings)


def _lru_dram(din):
    return dict(lru_conv_w=din("lru_conv_w", [128, 2, 8, 4]), lru_conv_b=din("lru_conv_b", [128, 16]),
                lru_b_a=din("lru_b_a", [128, 16]), lru_b_x=din("lru_b_x", [128, 16]), lru_lambda=din("lru_lambda", [128, 16]),
                lru_w_a=din("lru_w_a", [128, 2, 8, 128]), lru_w_x=din("lru_w_x", [128, 2, 8, 128]))


def build_A(with_ffn=True):
    nc = bass.Bass("TRN2", target_bir_lowering=False)

    def din(name, shape, dt=F32):
        return nc.dram_tensor(name, shape, dt, kind="ExternalInput").ap()

    def dout(name, shape, dt=F32):
        return nc.dram_tensor(name, shape, dt, kind="ExternalOutput").ap()

    xin = din("xin", [128, DC, TA])
    vmask = din("vmask", [128, TA])
    norms = din("norms", [128, 2 * DC])
    if with_ffn:
        wg = din("wg", [D, DFF]); wu = din("wu", [D, DFF]); wd = din("wd", [DFF, D])
    w_in = din("w_in", [D, DIN])
    pool_w = din("pool_w", [128, 4, 128])
    smallp = din("smallp", [128, 20 + 4 * 31])
    identd = din("ident", [128, 128])
    lru_d = _lru_dram(din)
    xout = dout("xout", [128, DC, TOWN]) if with_ffn else None
    ypc = dout("ypc", [128, 8, TOWN])
    gout = dout("gout", [128, 8, TOWN])
    xrout = dout("xrout", [128, 8, TA])
    car = dout("car", [128, 32])

    with ExitStack() as es:
        C = Ctx(nc, es)
        S = C.S
        sems = {e: es.enter_context(nc.semaphore("s_" + e)) for e in ENGS}
        rings = {e: [es.enter_context(nc.semaphore("r_%s%d" % (e, i))) for i in range(NRING)] for e in ("sp", "pool")}
        block = es.enter_context(nc.Block())

        h_sb = C.alloc("h", [DC, TA], BF16)
        g_sb = C.alloc("g", [2 * DC], F32)
        ones_f = C.alloc("ones_f", [128], F32)
        mx = C.mark()
        x_sb = C.alloc("x", [DC, TA], F32)
        S.op("dve", lambda h: h.memset(ones_f[:], 1.0), writes=["ones"])
        S.op("sp", lambda h: h.dma_start(out=g_sb[:], in_=norms), writes=["consts"], dma=True)
        emit_load_x(C, x_sb, xin, TA)
        if with_ffn:
            emit_rmsnorm(C, x_sb, g_sb, 0, h_sb, TA, ones_f)
            emit_ffn(C, x_sb, h_sb, TA, wg, wu, wd)
            for c in range(0, DC, 4):
                S.op("sp", lambda h, c=c: h.dma_start(out=xout[:, c:c + 4, :], in_=x_sb[:, c:c + 4, HALO:HALO + TOWN]),
                     reads=[("x", cc) for cc in range(c, c + 4)], dma=True)
        emit_rmsnorm(C, x_sb, g_sb, DC, h_sb, TA, ones_f)
        C.release(mx)

        sp_sb = C.alloc("smallp", [20 + 4 * 31], F32)
        S.op("sp", lambda h: h.dma_start(out=sp_sb[:], in_=smallp), writes=["consts"], dma=True)
        pw_sb = C.alloc("pool_w", [4, 128], BF16)
        S.op("pool", lambda h: h.dma_start(out=pw_sb[:], in_=pool_w), writes=["poolw"], dma=True)
        prm = load_lru_params(C, lru_d)
        vm = C.alloc("vm", [TA], F32)
        S.op("sp", lambda h: h.dma_start(out=vm[:], in_=vmask), writes=["vm"], dma=True)
        ones_t = C.alloc("ones_t", [TA], F32)
        S.op("pool", lambda h: h.memset(ones_t[:], 1.0), writes=["ones_t"])
        csm = C.alloc("csm", [TA], F32)
        csu = C.alloc("csu", [TA], F32)
        S.op("dve", lambda h: h.tensor_tensor_scan(out=csm[:], data0=ones_t[:], data1=vm[:], initial=0.0, op0=ALU.mult, op1=ALU.add),
             reads=["ones_t", "vm"], writes=["csm"])
        ident = C.alloc("ident", [128], BF16)
        identf = C.alloc("identf", [128], F32)
        S.op("sp", lambda h: h.dma_start(out=identf[:], in_=identd), writes=["identf"], dma=True)
        S.op("pool", lambda h: h.tensor_copy(out=ident[:], in_=identf[:]), reads=["identf"], writes=["ident"])

        NWR = 2
        wib = [C.alloc("wib%d" % i, [DC, 256], BF16) for i in range(NWR)]
        NU = 4
        ub = [C.alloc("ub%d" % i, [TA], F32) for i in range(NU)]
        tmpf = [C.alloc("tf%d" % i, [TA], F32) for i in range(6)]
        tmpb = C.alloc("tb", [TOWN], BF16)
        zb = C.alloc("zb", [4, TA], BF16)
        z2 = C.alloc("z2", [4, TOWN], F32)
        diag = C.alloc("diag", [31, 128], BF16)
        stg = [C.alloc("stg%d" % i, [TOWN], F32) for i in range(2)]
        accr = C.alloc("accr", [16, 2], F32)
        car_sb = C.alloc("car_sb", [32], F32)
        w_inv = w_in.rearrange("(kc p) n -> p kc n", p=128)
        TT = tiles_of(TA)
        st = {"w": 0, "u": 0, "stg": 0}

        def proj_pair(cols):
            wi = st["w"] % NWR
            st["w"] += 1
            for q, cc in enumerate(cols):
                S.op("pool", lambda h, q=q, cc=cc, wi=wi: h.dma_start(out=wib[wi][:, :, q * 128:(q + 1) * 128],
                                                                     in_=w_inv[:, :, cc * 128:(cc + 1) * 128]),
                     writes=[("wib", wi, q)], dma=True)
            res = []
            for q, cc in enumerate(cols):
                ui = st["u"] % NU
                st["u"] += 1
                res.append(ui)
                for (t0, n) in TT:
                    pi = C.next_ps()

                    def mm(h, q=q, wi=wi, pi=pi, t0=t0, n=n):
                        for kc in range(DC):
                            ins = h.matmul(C.ps[pi][:, 0:n], lhsT=wib[wi][:, kc, q * 128:(q + 1) * 128], rhs=h_sb[:, kc, t0:t0 + n],
                                           start=(kc == 0), stop=(kc == DC - 1))
                        return ins

                    S.op("pe", mm, reads=[("wib", wi, q)] + [("h", c) for c in range(DC)], writes=[("ps", pi)])
                    S.op("act", lambda h, ui=ui, pi=pi, t0=t0, n=n: h.activation(out=ub[ui][:, t0:t0 + n], in_=C.ps[pi][:, 0:n], func=AF.Copy),
                         reads=[("ps", pi)], writes=[("ub", ui)])
            return res

        def store(dst_ap, src_key, src_ap):
            S.op("sp", lambda h: h.dma_start(out=dst_ap, in_=src_ap), reads=[src_key], dma=True)

        WINS = (2, 4, 8, 16)
        for gp in range(2):
            uis = proj_pair([2 * gp, 2 * gp + 1])
            for q in range(2):
                g = 2 * gp + q
                w2 = WINS[g] // 2
                ui = uis[q]
                ssum, rc = tmpf[1], tmpf[2]
                S.op("dve", lambda h, ui=ui: h.tensor_tensor_scan(out=csu[:], data0=ones_t[:], data1=ub[ui][:], initial=0.0,
                                                                   op0=ALU.mult, op1=ALU.add),
                     reads=["ones_t", ("ub", ui)], writes=["csu"])
                lo = HALO - w2 - 1
                hi = HALO + w2 - 1
                S.op("dve", lambda h, lo=lo, hi=hi: h.tensor_tensor(out=ssum[:, 0:TOWN], in0=csu[:, hi:hi + TOWN], in1=csu[:, lo:lo + TOWN], op=ALU.subtract),
                     reads=["csu"], writes=[("tf", 1)])
                S.op("pool", lambda h, lo=lo, hi=hi: h.tensor_tensor(out=rc[:, 0:TOWN], in0=csm[:, hi:hi + TOWN], in1=csm[:, lo:lo + TOWN], op=ALU.subtract),
                     reads=["csm"], writes=[("tf", 2)])
                S.op("dve", lambda h: h.reciprocal(out=rc[:, 0:TOWN], in_=rc[:, 0:TOWN]), reads=[("tf", 2)], writes=[("tf", 2)])
                S.op("dve", lambda h: h.tensor_tensor(out=ssum[:, 0:TOWN], in0=ssum[:, 0:TOWN], in1=rc[:, 0:TOWN], op=ALU.mult),
                     reads=[("tf", 1), ("tf", 2)], writes=[("tf", 1)])
                S.op("dve", lambda h, ui=ui: h.tensor_tensor(out=tmpb[:], in0=ssum[:, 0:TOWN], in1=ub[ui][:, HALO:HALO + TOWN], op=ALU.subtract),
                     reads=[("tf", 1), ("ub", ui)], writes=["tmpb"])
                si = st["stg"] % 2
                st["stg"] += 1
                for (t0, n) in tiles_of(TOWN):
                    pi = C.next_ps()
                    S.op("pe", lambda h, g=g, pi=pi, t0=t0, n=n: h.matmul(C.ps[pi][:, 0:n], lhsT=pw_sb[:, g, :], rhs=tmpb[:, t0:t0 + n], start=True, stop=True),
                         reads=["tmpb", "poolw"], writes=[("ps", pi)])
                    S.op("act", lambda h, g=g, pi=pi, si=si, t0=t0, n=n: h.activation(out=stg[si][:, t0:t0 + n], in_=C.ps[pi][:, 0:n], func=AF.Copy,
                                                                                       scale=sp_sb[:, g:g + 1]),
                         reads=[("ps", pi), "consts"], writes=[("stg", si)])
                store(ypc[:, g, :], ("stg", si), stg[si][:])

        for c in range(4):
            uv, ug = proj_pair([4 + c, 8 + c])
            S.op("act", lambda h, ug=ug: h.activation(out=ub[ug][:], in_=ub[ug][:], func=AF.Sigmoid), reads=[("ub", ug)], writes=[("ub", ug)])
            S.op("dve", lambda h, c=c, uv=uv, ug=ug: h.tensor_tensor(out=zb[:, c, :], in0=ub[uv][:], in1=ub[ug][:], op=ALU.mult),
                 reads=[("ub", uv), ("ub", ug)], writes=[("zb", c)])
            for k in range(31):
                eng = "dve" if k % 2 == 0 else "pool"
                S.op(eng, lambda h, c=c, k=k: h.tensor_scalar(out=diag[:, k, :], in0=ident[:], scalar1=sp_sb[:, 20 + c * 31 + k:20 + c * 31 + k + 1],
                                                              scalar2=None, op0=ALU.mult),
                     reads=["ident", "consts"], writes=[("diag", k)])
            for (t0, n) in tiles_of(TOWN):
                pi = C.next_ps()

                def mmc(h, c=c, pi=pi, t0=t0, n=n):
                    for k in range(31):
                        ins = h.matmul(C.ps[pi][:, 0:n], lhsT=diag[:, k, :], rhs=zb[:, c, t0 + 1 + k:t0 + 1 + k + n], start=(k == 0), stop=(k == 30))
                    return ins

                S.op("pe", mmc, reads=[("zb", c)] + [("diag", k) for k in range(31)], writes=[("ps", pi)])
                S.op("act", lambda h, c=c, pi=pi, t0=t0, n=n: h.activation(out=z2[:, c, t0:t0 + n], in_=C.ps[pi][:, 0:n], func=AF.Identity,
                                                                          bias=sp_sb[:, 4 + c:5 + c]),
                     reads=[("ps", pi), "consts"], writes=[("z2", c, t0)])
        for (t0, n) in tiles_of(TOWN):
            pm = C.next_ps()
            pq = C.next_ps()
            for c in range(4):
                S.op("pe", lambda h, c=c, pm=pm, t0=t0, n=n: h.matmul(C.ps[pm][:, 0:n], lhsT=ones_f[:], rhs=z2[:, c, t0:t0 + n], start=(c == 0), stop=(c == 3)),
                     reads=[("z2", c, t0), "ones"], writes=[("ps", pm)])
            for c in range(4):
                S.op("act", lambda h, c=c, t0=t0, n=n: h.activation(out=tmpf[c][:, 0:n], in_=z2[:, c, t0:t0 + n], func=AF.Square),
                     reads=[("z2", c, t0)], writes=[("tf", c)])
                S.op("pe", lambda h, c=c, pq=pq, n=n: h.matmul(C.ps[pq][:, 0:n], lhsT=ones_f[:], rhs=tmpf[c][:, 0:n], start=(c == 0), stop=(c == 3)),
                     reads=[("tf", c), "ones"], writes=[("ps", pq)])
            mean, var = tmpf[4], tmpf[5]
            S.op("dve", lambda h, pm=pm, n=n: h.tensor_scalar(out=mean[:, 0:n], in0=C.ps[pm][:, 0:n], scalar1=1.0 / 512, scalar2=None, op0=ALU.mult),
                 reads=[("ps", pm)], writes=[("tf", 4)])
            S.op("dve", lambda h, pq=pq, n=n: h.tensor_scalar(out=var[:, 0:n], in0=C.ps[pq][:, 0:n], scalar1=1.0 / 512, scalar2=LN_EPS, op0=ALU.mult, op1=ALU.add),
                 reads=[("ps", pq)], writes=[("tf", 5)])
            S.op("dve", lambda h, n=n: h.tensor_tensor(out=tmpf[0][:, 0:n], in0=mean[:, 0:n], in1=mean[:, 0:n], op=ALU.mult),
                 reads=[("tf", 4)], writes=[("tf", 0)])
            S.op("dve", lambda h, n=n: h.tensor_tensor(out=var[:, 0:n], in0=var[:, 0:n], in1=tmpf[0][:, 0:n], op=ALU.subtract),
                 reads=[("tf", 5), ("tf", 0)], writes=[("tf", 5)])
            S.op("act", lambda h, n=n: h.activation(out=var[:, 0:n], in_=var[:, 0:n], func=AF.Sqrt), reads=[("tf", 5)], writes=[("tf", 5)])
            S.op("dve", lambda h, n=n: h.reciprocal(out=var[:, 0:n], in_=var[:, 0:n]), reads=[("tf", 5)], writes=[("tf", 5)])
            for c in range(4):
                t1 = tmpf[c % 2]
                S.op("dve", lambda h, c=c, t1=t1, t0=t0, n=n: h.tensor_tensor(out=t1[:, 0:n], in0=z2[:, c, t0:t0 + n], in1=mean[:, 0:n], op=ALU.subtract),
                     reads=[("z2", c, t0), ("tf", 4)], writes=[("tf", c % 2)])
                S.op("dve", lambda h, c=c, t1=t1, n=n: h.tensor_tensor(out=t1[:, 0:n], in0=t1[:, 0:n], in1=var[:, 0:n], op=ALU.mult),
                     reads=[("tf", c % 2), ("tf", 5)], writes=[("tf", c % 2)])
                S.op("act", lambda h, c=c, t1=t1, t0=t0, n=n: h.activation(out=z2[:, c, t0:t0 + n], in_=t1[:, 0:n], func=AF.Silu,
                                                                          scale=sp_sb[:, 8 + c:9 + c], bias=sp_sb[:, 12 + c:13 + c]),
                     reads=[("tf", c % 2), "consts"], writes=[("z2", c, t0)])
        for c in range(4):
            S.op("sp", lambda h, c=c: h.dma_start(out=ypc[:, 4 + c, :], in_=z2[:, c, :]),
                 reads=[("z2", c, t0) for (t0, n) in tiles_of(TOWN)], dma=True)

        for gp in range(4):
            uis = proj_pair([12 + 2 * gp, 13 + 2 * gp])
            for q in range(2):
                c = 2 * gp + q
                ui = uis[q]
                t1 = tmpf[0][:, 0:TOWN]
                si = st["stg"] % 2
                st["stg"] += 1
                uo = ub[ui][:, HALO:HALO + TOWN]
                S.op("pool", lambda h, uo=uo, t1=t1: h.tensor_tensor(out=t1, in0=uo, in1=uo, op=ALU.mult), reads=[("ub", ui)], writes=[("tf", 0)])
                S.op("dve", lambda h, t1=t1: h.tensor_scalar(out=t1, in0=t1, scalar1=0.044715, scalar2=1.0, op0=ALU.mult, op1=ALU.add),
                     reads=[("tf", 0)], writes=[("tf", 0)])
                S.op("dve", lambda h, uo=uo, t1=t1: h.tensor_tensor(out=t1, in0=t1, in1=uo, op=ALU.mult), reads=[("tf", 0), ("ub", ui)], writes=[("tf", 0)])
                S.op("act", lambda h, t1=t1: h.activation(out=t1, in_=t1, func=AF.Sigmoid, scale=1.5957691216057308), reads=[("tf", 0)], writes=[("tf", 0)])
                S.op("dve", lambda h, uo=uo, si=si, t1=t1: h.tensor_tensor(out=stg[si][:], in0=t1, in1=uo, op=ALU.mult),
                     reads=[("tf", 0), ("ub", ui)], writes=[("stg", si)])
                store(gout[:, c, :], ("stg", si), stg[si][:])

        S.op("dve", lambda h: h.memset(accr[:], 0.0), writes=["accr"])
        tmp5 = [(tmpf[i], ("tf", i)) for i in range(5)]
        hout = tmpf[5][:, 0:TOWN]
        for gp in range(4):
            uis = proj_pair([20 + 2 * gp, 21 + 2 * gp])
            for q in range(2):
                c = 2 * gp + q
                ui = uis[q]
                store(xrout[:, c, :], ("ub", ui), ub[ui][:])
                for d in range(2):
                    col = d * 8 + c
                    emit_lru_head(C, c, d, ub[ui], ("ub", ui), HALO, prm, tmp5, tmpb, 0.0, hout, ("tf", 5), accr[:, col, :])
                    last = TOWN - 1 if d == 0 else 0
                    S.op("pool", lambda h, col=col, last=last: h.tensor_copy(out=car_sb[:, 16 + col:17 + col], in_=hout[:, last:last + 1]),
                         reads=[("tf", 5)], writes=["car"])
        S.op("dve", lambda h: h.tensor_tensor(out=car_sb[:, 0:16], in0=accr[:, :, 0], in1=accr[:, :, 1], op=ALU.add),
             reads=["accr"], writes=["car"])
        S.op("sp", lambda h: h.dma_start(out=car, in_=car_sb[:]), reads=["car"], dma=True)
        _finish(nc, C, block, sems, rings)
    return nc


def build_B(final, next_ffn=False):
    nc = bass.Bass("TRN2", target_bir_lowering=False)

    def din(name, shape, dt=F32):
        return nc.dram_tensor(name, shape, dt, kind="ExternalInput").ap()

    def dout(name, shape, dt=F32):
        return nc.dram_tensor(name, shape, dt, kind="ExternalOutput").ap()

    xin = din("xin", [128, DC, TOWN])
    ypc = din("ypc", [128, 8, TOWN])
    gin = din("gin", [128, 8, TOWN])
    xrin = din("xrin", [128, 8, TA])
    carall = din("carall", [128, NCORES, 32])
    onehot = din("onehot", [128, NCORES])
    norms = din("norms", [128, 3 * DC])
    w_out = din("w_out", [D, D])
    wg = din("wg", [D, DFF]); wu = din("wu", [D, DFF]); wd = din("wd", [DFF, D])
    if next_ffn:
        wgn = din("wgn", [D, DFF]); wun = din("wun", [D, DFF]); wdn = din("wdn", [DFF, D])
    lru_d = _lru_dram(din)
    xout = dout("xout", [128, DC, TOWN])

    with ExitStack() as es:
        C = Ctx(nc, es)
        S = C.S
        sems = {e: es.enter_context(nc.semaphore("s_" + e)) for e in ENGS}
        rings = {e: [es.enter_context(nc.semaphore("r_%s%d" % (e, i))) for i in range(NRING)] for e in ("sp", "pool")}
        block = es.enter_context(nc.Block())

        x_sb = C.alloc("x", [DC, TOWN], F32)
        g_sb = C.alloc("g", [3 * DC], F32)
        ones_f = C.alloc("ones_f", [128], F32)
        S.op("dve", lambda h: h.memset(ones_f[:], 1.0), writes=["ones"])
        S.op("sp", lambda h: h.dma_start(out=g_sb[:], in_=norms), writes=["consts"], dma=True)
        m_y = C.mark()
        y_sb = C.alloc("y", [DC, TOWN], BF16)
        for c in range(0, 8, 4):
            S.op("pool", lambda h, c=c: h.dma_start(out=y_sb[:, c:c + 4, :], in_=ypc[:, c:c + 4, :]),
                 writes=[("y", cc) for cc in range(c, c + 4)], dma=True)
        m_l = C.mark()
        prm = load_lru_params(C, lru_d)
        ca = C.alloc("carall", [NCORES, 32], F32)
        oh = C.alloc("onehot", [NCORES], F32)
        S.op("sp", lambda h: h.dma_start(out=ca[:], in_=carall), writes=["carall"], dma=True)
        S.op("sp", lambda h: h.dma_start(out=oh[:], in_=onehot), writes=["consts"], dma=True)
        at = C.alloc("atot", [NCORES, 16], F32)
        for j in range(NCORES):
            S.op("dve", lambda h, j=j: h.tensor_tensor(out=at[:, j, :], in0=ca[:, j, 0:16], in1=prm["c1"][:], op=ALU.mult),
                 reads=["carall", "c1"], writes=[("at", j)])
            S.op("act", lambda h, j=j: h.activation(out=at[:, j, :], in_=at[:, j, :], func=AF.Exp), reads=[("at", j)], writes=[("at", j)])
        cin = C.alloc("cin", [NCORES, 16], F32)
        S.op("dve", lambda h: h.memset(cin[:], 0.0), writes=["cin"])
        for j in range(1, NCORES):
            S.op("dve", lambda h, j=j: h.tensor_tensor(out=cin[:, j, 0:8], in0=at[:, j - 1, 0:8], in1=cin[:, j - 1, 0:8], op=ALU.mult),
                 reads=[("at", j - 1), "cin"], writes=["cin"])
            S.op("dve", lambda h, j=j: h.tensor_tensor(out=cin[:, j, 0:8], in0=cin[:, j, 0:8], in1=ca[:, j - 1, 16:24], op=ALU.add),
                 reads=["cin", "carall"], writes=["cin"])
        for j in range(NCORES - 2, -1, -1):
            S.op("dve", lambda h, j=j: h.tensor_tensor(out=cin[:, j, 8:16], in0=at[:, j + 1, 8:16], in1=cin[:, j + 1, 8:16], op=ALU.mult),
                 reads=[("at", j + 1), "cin"], writes=["cin"])
            S.op("dve", lambda h, j=j: h.tensor_tensor(out=cin[:, j, 8:16], in0=cin[:, j, 8:16], in1=ca[:, j + 1, 24:32], op=ALU.add),
                 reads=["cin", "carall"], writes=["cin"])
        mine = C.alloc("mine", [16], F32)
        S.op("dve", lambda h: h.tensor_scalar(out=mine[:], in0=cin[:, 0, :], scalar1=oh[:, 0:1], scalar2=None, op0=ALU.mult),
             reads=["cin", "consts"], writes=["carry"])
        for j in range(1, NCORES):
            S.op("dve", lambda h, j=j: h.scalar_tensor_tensor(out=mine[:], in0=cin[:, j, :], scalar=oh[:, j:j + 1], in1=mine[:],
                                                               op0=ALU.mult, op1=ALU.add),
                 reads=["cin", "consts", "carry"], writes=["carry"])

        xrb = [C.alloc("xrb%d" % i, [TA], F32) for i in range(2)]
        gb = [C.alloc("gb%d" % i, [TOWN], F32) for i in range(2)]
        tmpf = [C.alloc("tf%d" % i, [TA], F32) for i in range(5)]
        tmpb = C.alloc("tb", [TOWN], BF16)
        hf = C.alloc("hf", [TOWN], F32)
        hb = C.alloc("hb", [TOWN], F32)
        tmp5 = [(tmpf[i], ("tf", i)) for i in range(5)]
        for c in range(8):
            bi = c % 2
            S.op("sp", lambda h, c=c, bi=bi: h.dma_start(out=xrb[bi][:], in_=xrin[:, c, :]), writes=[("xrb", bi)], dma=True)
            S.op("sp", lambda h, c=c, bi=bi: h.dma_start(out=gb[bi][:], in_=gin[:, c, :]), writes=[("gb", bi)], dma=True)
            if c == 1:
                emit_load_x(C, x_sb, xin, TOWN)
            emit_lru_head(C, c, 0, xrb[bi], ("xrb", bi), HALO, prm, tmp5, tmpb, mine[:, c:c + 1], hf[:], "hf", None)
            emit_lru_head(C, c, 1, xrb[bi], ("xrb", bi), HALO, prm, tmp5, tmpb, mine[:, 8 + c:9 + c], hb[:], "hb", None)
            S.op("dve", lambda h: h.tensor_tensor(out=hf[:], in0=hf[:], in1=hb[:], op=ALU.add), reads=["hf", "hb"], writes=["hf"])
            S.op("dve", lambda h, c=c, bi=bi: h.tensor_tensor(out=y_sb[:, 8 + c, :], in0=hf[:], in1=gb[bi][:], op=ALU.mult),
                 reads=["hf", ("gb", bi)], writes=[("y", 8 + c)])
        C.release(m_l)

        NWR = 3
        wob = [C.alloc("wob%d" % i, [DC, 256], BF16) for i in range(NWR)]
        w_ov = w_out.rearrange("(kc p) n -> p kc n", p=128)
        for n2 in range(DC // 2):
            wi = n2 % NWR
            S.op("pool", lambda h, n2=n2, wi=wi: h.dma_start(out=wob[wi][:], in_=w_ov[:, :, n2 * 256:(n2 + 1) * 256]),
                 writes=[("wob", wi)], dma=True)
            for nn in range(2):
                nchunk = 2 * n2 + nn
                for (t0, n) in tiles_of(TOWN):
                    po = C.next_ps()

                    def mmo(h, wi=wi, nn=nn, po=po, t0=t0, n=n):
                        for kc in range(DC):
                            ins = h.matmul(C.ps[po][:, 0:n], lhsT=wob[wi][:, kc, nn * 128:(nn + 1) * 128], rhs=y_sb[:, kc, t0:t0 + n],
                                           start=(kc == 0), stop=(kc == DC - 1))
                        return ins

                    S.op("pe", mmo, reads=[("wob", wi)] + [("y", kc) for kc in range(DC)], writes=[("ps", po)])
                    S.op("dve", lambda h, po=po, nchunk=nchunk, t0=t0, n=n: h.tensor_tensor(
                        out=x_sb[:, nchunk, t0:t0 + n], in0=C.ps[po][:, 0:n], in1=x_sb[:, nchunk, t0:t0 + n], op=ALU.add),
                        reads=[("ps", po), ("x", nchunk)], writes=[("x", nchunk)])
        C.release(m_y)

        h_sb = C.alloc("h", [DC, TOWN], BF16)
        emit_rmsnorm(C, x_sb, g_sb, 0, h_sb, TOWN, ones_f)
        emit_ffn(C, x_sb, h_sb, TOWN, wg, wu, wd)
        if next_ffn:
            emit_rmsnorm(C, x_sb, g_sb, 2 * DC, h_sb, TOWN, ones_f)
            emit_ffn(C, x_sb, h_sb, TOWN, wgn, wun, wdn)
        if final:
            C.release(C.mark())
            C.off = m_y
            o_sb = C.alloc("o", [DC, TOWN], F32)
            emit_rmsnorm(C, x_sb, g_sb, DC, None, TOWN, ones_f, hkey="o", out_f32=o_sb)
            for c in range(0, DC, 4):
                S.op("sp", lambda h, c=c: h.dma_start(out=xout[:, c:c + 4, :], in_=o_sb[:, c:c + 4, :]),
                     reads=[("o", cc) for cc in range(c, c + 4)], dma=True)
        else:
            for c in range(0, DC, 4):
                S.op("sp", lambda h, c=c: h.dma_start(out=xout[:, c:c + 4, :], in_=x_sb[:, c:c + 4, :]),
                     reads=[("x", cc) for cc in range(c, c + 4)], dma=True)
        _finish(nc, C, block, sems, rings)
    return nc


_CACHE = {}


def _prog(name):
    if name not in _CACHE:
        _CACHE[name] = {"A": lambda: build_A(True), "An": lambda: build_A(False), "Bn": lambda: build_B(False, True),
                        "Bf": lambda: build_B(True, False)}[name]()
    return _CACHE[name]


def _fm(v):
    v = np.asarray(v, dtype=np.float32)
    return np.ascontiguousarray(v.reshape(-1, 128).T)


def _run(nc, in_maps):
    res = run_bass_kernel_spmd(nc, in_maps, core_ids=list(range(NCORES)))
    return res.results


def kernel(x, norm_ffn1, ffn1_w_gate, ffn1_w_up, ffn1_w_down, norm_mix, w_in,
           pool_w, pool_scale, conv_dw_w, conv_dw_b, conv_ln_g, conv_ln_b,
           lru_conv_w, lru_conv_b, lru_w_a, lru_b_a, lru_w_x, lru_b_x, lru_lambda,
           w_out, norm_ffn2, ffn2_w_gate, ffn2_w_up, ffn2_w_down, norm_final, _nlayers=DEPTH, _debug=None):
    f32 = np.float32
    A = lambda a: np.ascontiguousarray(np.asarray(a, dtype=f32))
    x = A(x)
    xfm = np.ascontiguousarray(x.reshape(SEQ, DC, 128).transpose(2, 1, 0))
    ident = np.eye(128, dtype=f32)
    vfull = np.zeros((SEQ + 2 * HALO,), f32)
    vfull[HALO:HALO + SEQ] = 1.0
    onehots = [np.ascontiguousarray(np.broadcast_to(np.eye(NCORES, dtype=f32)[c][None, :], (128, NCORES))) for c in range(NCORES)]
    for l in range(_nlayers):
        lp = dict(
            lru_conv_w=np.ascontiguousarray(A(lru_conv_w[l]).reshape(2, 4, 8, 128).transpose(3, 0, 2, 1)),
            lru_conv_b=_fm(lru_conv_b[l]), lru_b_a=_fm(lru_b_a[l]), lru_b_x=_fm(lru_b_x[l]), lru_lambda=_fm(lru_lambda[l]),
            lru_w_a=np.ascontiguousarray(A(lru_w_a[l]).transpose(2, 0, 1, 3)),
            lru_w_x=np.ascontiguousarray(A(lru_w_x[l]).transpose(2, 0, 1, 3)),
        )
        smallp = np.zeros((128, 20 + 4 * 31), f32)
        smallp[:, 0:4] = _fm(pool_scale[l])
        smallp[:, 4:8] = _fm(conv_dw_b[l])
        smallp[:, 8:12] = _fm(conv_ln_g[l])
        smallp[:, 12:16] = _fm(conv_ln_b[l])
        smallp[:, 20:] = A(conv_dw_w[l]).reshape(31, 4, 128).transpose(2, 1, 0).reshape(128, 124)
        commonA = dict(norms=np.concatenate([_fm(norm_ffn1[l]), _fm(norm_mix[l])], axis=1), w_in=A(w_in[l]),
                       pool_w=np.ascontiguousarray(A(pool_w[l]).transpose(1, 0, 2)), smallp=smallp, ident=ident, **lp)
        if l == 0:
            commonA.update(wg=A(ffn1_w_gate[l]), wu=A(ffn1_w_up[l]), wd=A(ffn1_w_down[l]))
        xpad = np.zeros((128, DC, SEQ + 2 * HALO), f32)
        xpad[:, :, HALO:HALO + SEQ] = xfm
        in_maps = []
        for c in range(NCORES):
            s = c * TOWN
            in_maps.append(dict(xin=np.ascontiguousarray(xpad[:, :, s:s + TA]),
                                vmask=np.ascontiguousarray(np.broadcast_to(vfull[s:s + TA][None, :], (128, TA))), **commonA))
        ra = _run(_prog("A" if l == 0 else "An"), in_maps)
        if _debug is not None:
            _debug["A%d" % l] = ra
        carall = np.ascontiguousarray(np.stack([ra[c]["car"] for c in range(NCORES)], axis=1))
        last = (l == DEPTH - 1)
        nl1 = min(l + 1, DEPTH - 1)
        commonB = dict(norms=np.concatenate([_fm(norm_ffn2[l]), _fm(norm_final), _fm(norm_ffn1[nl1])], axis=1), w_out=A(w_out[l]),
                       wg=A(ffn2_w_gate[l]), wu=A(ffn2_w_up[l]), wd=A(ffn2_w_down[l]), carall=carall, **lp)
        if not last:
            commonB.update(wgn=A(ffn1_w_gate[l + 1]), wun=A(ffn1_w_up[l + 1]), wdn=A(ffn1_w_down[l + 1]))
        in_maps = []
        for c in range(NCORES):
            xin_b = ra[c]["xout"] if l == 0 else np.ascontiguousarray(xfm[:, :, c * TOWN:(c + 1) * TOWN])
            in_maps.append(dict(xin=xin_b, ypc=ra[c]["ypc"], gin=ra[c]["gout"], xrin=ra[c]["xrout"],
                                onehot=onehots[c], **commonB))
        rb = _run(_prog("Bf" if last else "Bn"), in_maps)
        if _debug is not None:
            _debug["B%d" % l] = rb
        xfm = np.concatenate([rb[c]["xout"] for c in range(NCORES)], axis=2)
    out = np.ascontiguousarray(xfm.transpose(2, 1, 0)).reshape(1, SEQ, D)
    return out.astype(np.float32)
```

```python
import numpy as np
from contextlib import ExitStack
import concourse.bass as bass
import concourse.mybir as mybir
from concourse.bass_utils import run_bass_kernel_spmd

F32 = mybir.dt.float32
BF16 = mybir.dt.bfloat16
AF = mybir.ActivationFunctionType
ALU = mybir.AluOpType

ENGS = ("pe", "act", "dve", "pool", "sp")
NRING = 6
SAME_ENG_DIST = 4

NCORES = 8
D = 2048
DC = 16
DFF = 5632
FC = 44
DIN = 3584
SEQ = 8192
TOWN = 1024
HALO = 16
TA = TOWN + 2 * HALO
DEPTH = 4
RMS_EPS = 1e-6
LN_EPS = 1e-5


class _Op:
    __slots__ = ("id", "eng", "fn", "deps", "dma", "seq", "pos", "ring", "rval", "ndep")


class Sched:
    def __init__(self):
        self.ops = []
        self.by_eng = {e: [] for e in ENGS}
        self.last_w = {}
        self.readers = {}
        self.pending_bar = {}
        self.dma_since_bar = []

    def op(self, eng, fn, reads=(), writes=(), dma=False):
        o = _Op()
        o.id = len(self.ops)
        o.eng = eng
        o.fn = fn
        o.dma = dma
        o.seq = None
        o.ndep = 0
        deps = set()
        for r in reads:
            w = self.last_w.get(r)
            if w is not None:
                deps.add(w)
        for k in writes:
            w = self.last_w.get(k)
            if w is not None:
                deps.add(w)
            rd = self.readers.get(k)
            if rd:
                deps.update(rd.values())
        b = self.pending_bar.pop(eng, None)
        if b:
            deps.update(b)
        o.deps = deps
        for r in reads:
            d = self.readers.setdefault(r, {})
            d[("dma", o.id) if dma else eng] = o.id
        for k in writes:
            self.last_w[k] = o.id
            self.readers[k] = {}
        o.pos = len(self.by_eng[eng])
        self.ops.append(o)
        self.by_eng[eng].append(o)
        if dma:
            self.dma_since_bar.append(o.id)
        return o

    def barrier(self):
        b = set(self.dma_since_bar)
        for e in ENGS:
            if self.by_eng[e]:
                b.add(self.by_eng[e][-1].id)
        for e in ENGS:
            self.pending_bar.setdefault(e, set()).update(b)
        self.dma_since_bar = []

    def prepare(self):
        ops = self.ops
        for o in ops:
            for d in o.deps:
                p = ops[d]
                if p.dma:
                    continue
                if p.eng == o.eng and (p.eng == "pe" or o.pos - p.pos >= SAME_ENG_DIST):
                    continue
                p.ndep += 1
        for e in ENGS:
            s = 0
            k = 0
            for o in self.by_eng[e]:
                if o.dma:
                    o.ring = k % NRING
                    o.rval = 16 * (k // NRING + 1)
                    k += 1
                elif o.ndep:
                    s += 1
                    o.seq = s

    def run(self, e, h, sems, rings):
        ops = self.ops
        waited = {}
        for o in self.by_eng[e]:
            need = {}
            for d in o.deps:
                p = ops[d]
                if p.dma:
                    sem = rings[p.eng][p.ring]
                    val = p.rval
                else:
                    if p.seq is None:
                        continue
                    if p.eng == e and (e == "pe" or o.pos - p.pos >= SAME_ENG_DIST):
                        continue
                    sem = sems[p.eng]
                    val = p.seq
                k = id(sem)
                if k not in need or need[k][1] < val:
                    need[k] = (sem, val)
            if o.dma and o.rval > 16:
                sem = rings[e][o.ring]
                k = id(sem)
                v = o.rval - 16
                if k not in need or need[k][1] < v:
                    need[k] = (sem, v)
            for k, (sem, val) in need.items():
                if waited.get(k, 0) >= val:
                    continue
                waited[k] = val
                h.wait_ge(sem, val)
            ins = o.fn(h)
            if o.dma:
                ins.then_inc(rings[e][o.ring], 16)
            elif o.seq is not None:
                ins.then_inc(sems[e], 1)
        last = {}
        for o in self.by_eng[e]:
            if o.dma:
                last[o.ring] = o.rval
        for r, v in last.items():
            h.wait_ge(rings[e][r], v)


class Ctx:
    def __init__(self, nc, es, arena_bytes=200 * 1024):
        self.nc = nc
        self.S = Sched()
        self.arena = es.enter_context(nc.sbuf_tensor("arena", [128, arena_bytes // 2], BF16))
        self.arena_bytes = arena_bytes
        self.off = 0
        self.ps = [es.enter_context(nc.psum_tensor("ps%d" % i, [128, 512], F32)) for i in range(8)]
        self.psi = 0
        self.uid = 0

    def alloc(self, name, free_shape, dt):
        n = int(np.prod(free_shape))
        esz = 2 if dt == BF16 else 4
        nbytes = (n * esz + 63) // 64 * 64
        assert self.off + nbytes <= self.arena_bytes, (name, self.off, nbytes, self.arena_bytes)
        v = self.arena[:, self.off // 2:(self.off + n * esz) // 2]
        if dt != BF16:
            v = v.bitcast(dt)
        if len(free_shape) == 2:
            v = v.rearrange("p (a b) -> p a b", a=free_shape[0])
        elif len(free_shape) == 3:
            v = v.rearrange("p (a b c) -> p a b c", a=free_shape[0], b=free_shape[1])
        self.off += nbytes
        return v

    def mark(self):
        return self.off

    def release(self, m):
        self.S.barrier()
        self.off = m

    def next_ps(self):
        i = self.psi
        self.psi = (self.psi + 1) % 8
        return i

    def key(self, base):
        self.uid += 1
        return "%s#%d" % (base, self.uid)


def tiles_of(T):
    out = []
    t = 0
    while t < T:
        n = min(512, T - t)
        out.append((t, n))
        t += n
    return out


def emit_load_x(C, x_sb, x_dram, T):
    S = C.S
    for c in range(0, DC, 4):
        S.op("sp", lambda h, c=c: h.dma_start(out=x_sb[:, c:c + 4, :], in_=x_dram[:, c:c + 4, :]),
             writes=[("x", cc) for cc in range(c, c + 4)], dma=True)


def emit_rmsnorm(C, x_sb, g_sb, gcol0, h_sb, T, ones_f, hkey="h", out_f32=None):
    S = C.S
    m = C.mark()
    NSQ = 4
    sq = [C.alloc("sq%d" % i, [512], BF16) for i in range(NSQ)]
    ones_b = C.alloc("ones_b", [128], BF16)
    S.op("dve", lambda h: h.memset(ones_b[:], 1.0), writes=["ones_b"])
    rstd = C.alloc("rstd", [T], F32)
    epsc = C.alloc("epsc", [1], F32)
    S.op("dve", lambda h: h.memset(epsc[:], RMS_EPS), writes=["epsc"])
    for (t0, n) in tiles_of(T):
        pi = C.next_ps()
        ps = C.ps[pi]
        for c in range(DC):
            i = c % NSQ
            eng = "act" if c % 2 == 0 else "dve"
            if eng == "act":
                S.op("act", lambda h, c=c, i=i, t0=t0, n=n: h.activation(out=sq[i][:, 0:n], in_=x_sb[:, c, t0:t0 + n], func=AF.Square),
                     reads=[("x", c)], writes=[("sq", i)])
            else:
                S.op("dve", lambda h, c=c, i=i, t0=t0, n=n: h.tensor_tensor(out=sq[i][:, 0:n], in0=x_sb[:, c, t0:t0 + n],
                                                                  in1=x_sb[:, c, t0:t0 + n], op=ALU.mult),
                     reads=[("x", c)], writes=[("sq", i)])
            S.op("pe", lambda h, c=c, i=i, ps=ps, n=n: h.matmul(ps[:, 0:n], lhsT=ones_b[:], rhs=sq[i][:, 0:n],
                                                     start=(c == 0), stop=(c == DC - 1)),
                 reads=[("sq", i), "ones_b"], writes=[("ps", pi)])
        S.op("act", lambda h, t0=t0, n=n, ps=ps: h.activation(out=rstd[:, t0:t0 + n], in_=ps[:, 0:n], func=AF.Sqrt, scale=1.0 / D,
                                                               bias=epsc[:, 0:1]),
             reads=[("ps", pi), "epsc"], writes=[("rstd", t0)])
        S.op("dve", lambda h, t0=t0, n=n: h.reciprocal(out=rstd[:, t0:t0 + n], in_=rstd[:, t0:t0 + n]),
             reads=[("rstd", t0)], writes=[("rstd", t0)])
        for c in range(DC):
            eng = "dve"
            dst = (h_sb[:, c, t0:t0 + n] if out_f32 is None else out_f32[:, c, t0:t0 + n])
            S.op(eng, lambda h, c=c, dst=dst, t0=t0, n=n: h.scalar_tensor_tensor(out=dst, in0=x_sb[:, c, t0:t0 + n],
                                                                      scalar=g_sb[:, gcol0 + c:gcol0 + c + 1],
                                                                      in1=rstd[:, t0:t0 + n], op0=ALU.mult, op1=ALU.mult),
                 reads=[("x", c), ("rstd", t0), "consts"], writes=[(hkey, c)])
    C.release(m)


def emit_ffn(C, x_sb, h_sb, T, wg, wu, wd):
    S = C.S
    TT = tiles_of(T)
    NPH = 4
    JP = FC // NPH
    m = C.mark()
    hid = C.alloc("hid", [JP, T], BF16)
    NWR = 4
    NWD = 3
    wgb = [C.alloc("wgb%d" % i, [DC, 128], BF16) for i in range(NWR)]
    wub = [C.alloc("wub%d" % i, [DC, 128], BF16) for i in range(NWR)]
    wdb = [C.alloc("wdb%d" % i, [JP, 256], BF16) for i in range(NWD)]
    sg = [C.alloc("sg%d" % i, [512], F32) for i in range(3)]
    wgv = wg.rearrange("(kc p) n -> p kc n", p=128)
    wuv = wu.rearrange("(kc p) n -> p kc n", p=128)
    wdv = wd.rearrange("(j p) n -> p j n", p=128)
    wcnt = 0
    scnt = 0
    dcnt = 0
    for ph in range(NPH):
        for jj in range(JP):
            j = ph * JP + jj
            wi = wcnt % NWR
            wcnt += 1
            S.op("pool", lambda h, j=j, wi=wi: h.dma_start(out=wgb[wi][:], in_=wgv[:, :, j * 128:(j + 1) * 128]),
                 writes=[("wgb", wi)], dma=True)
            S.op("pool", lambda h, j=j, wi=wi: h.dma_start(out=wub[wi][:], in_=wuv[:, :, j * 128:(j + 1) * 128]),
                 writes=[("wub", wi)], dma=True)
            for (t0, n) in TT:
                pg = C.next_ps()
                pu = C.next_ps()

                def mm_g(h, wi=wi, pg=pg, t0=t0, n=n):
                    for kc in range(DC):
                        ins = h.matmul(C.ps[pg][:, 0:n], lhsT=wgb[wi][:, kc, :], rhs=h_sb[:, kc, t0:t0 + n],
                                       start=(kc == 0), stop=(kc == DC - 1))
                    return ins

                def mm_u(h, wi=wi, pu=pu, t0=t0, n=n):
                    for kc in range(DC):
                        ins = h.matmul(C.ps[pu][:, 0:n], lhsT=wub[wi][:, kc, :], rhs=h_sb[:, kc, t0:t0 + n],
                                       start=(kc == 0), stop=(kc == DC - 1))
                    return ins

                hreads = [("h", c) for c in range(DC)]
                S.op("pe", mm_g, reads=[("wgb", wi)] + hreads, writes=[("ps", pg)])
                S.op("pe", mm_u, reads=[("wub", wi)] + hreads, writes=[("ps", pu)])
                si = scnt % 3
                scnt += 1
                S.op("act", lambda h, si=si, pg=pg, n=n: h.activation(out=sg[si][:, 0:n], in_=C.ps[pg][:, 0:n], func=AF.Silu),
                     reads=[("ps", pg)], writes=[("sg", si)])
                S.op("dve", lambda h, si=si, pu=pu, jj=jj, t0=t0, n=n: h.tensor_tensor(
                    out=hid[:, jj, t0:t0 + n], in0=sg[si][:, 0:n], in1=C.ps[pu][:, 0:n], op=ALU.mult),
                    reads=[("sg", si), ("ps", pu)], writes=[("hid", jj, t0)])
        for n2 in range(DC // 2):
            di = dcnt % NWD
            dcnt += 1
            S.op("pool", lambda h, di=di, n2=n2, ph=ph: h.dma_start(
                out=wdb[di][:], in_=wdv[:, ph * JP:(ph + 1) * JP, n2 * 256:(n2 + 1) * 256]),
                writes=[("wdb", di)], dma=True)
            for nn in range(2):
                nchunk = n2 * 2 + nn
                for (t0, n) in TT:
                    po = C.next_ps()

                    def mm_d(h, di=di, nn=nn, po=po, t0=t0, n=n):
                        for jj in range(JP):
                            ins = h.matmul(C.ps[po][:, 0:n], lhsT=wdb[di][:, jj, nn * 128:(nn + 1) * 128],
                                           rhs=hid[:, jj, t0:t0 + n], start=(jj == 0), stop=(jj == JP - 1))
                        return ins

                    S.op("pe", mm_d, reads=[("wdb", di)] + [("hid", jj, t0) for jj in range(JP)], writes=[("ps", po)])
                    S.op("dve", lambda h, po=po, nchunk=nchunk, t0=t0, n=n: h.scalar_tensor_tensor(
                        out=x_sb[:, nchunk, t0:t0 + n], in0=C.ps[po][:, 0:n], scalar=0.5, in1=x_sb[:, nchunk, t0:t0 + n],
                        op0=ALU.mult, op1=ALU.add),
                        reads=[("ps", po), ("x", nchunk)], writes=[("x", nchunk)])
    C.release(m)


def emit_softplus_c(C, lam_sb, c1, c2, ncols):
    S = C.S
    m = C.mark()
    e = C.alloc("sp_e", [ncols], F32)
    s = C.alloc("sp_s", [ncols], F32)
    s2 = C.alloc("sp_s2", [ncols], F32)
    acc = C.alloc("sp_acc", [ncols], F32)
    S.op("act", lambda h: h.activation(out=e[:], in_=lam_sb, func=AF.Exp, scale=-1.0), reads=["consts"], writes=["sp_e"])
    S.op("dve", lambda h: h.tensor_scalar(out=s[:], in0=e[:], scalar1=2.0, scalar2=None, op0=ALU.add),
         reads=["sp_e"], writes=["sp_s"])
    S.op("dve", lambda h: h.reciprocal(out=s[:], in_=s[:]), reads=["sp_s"], writes=["sp_s"])
    S.op("dve", lambda h: h.tensor_tensor(out=s[:], in0=e[:], in1=s[:], op=ALU.mult), reads=["sp_e", "sp_s"], writes=["sp_s"])
    S.op("dve", lambda h: h.tensor_tensor(out=s2[:], in0=s[:], in1=s[:], op=ALU.mult), reads=["sp_s"], writes=["sp_s2"])
    NT = 9
    S.op("dve", lambda h: h.memset(acc[:], 1.0 / (2 * NT + 1)), writes=["sp_acc"])
    for k in range(NT - 1, -1, -1):
        S.op("dve", lambda h, k=k: h.tensor_tensor(out=acc[:], in0=acc[:], in1=s2[:], op=ALU.mult),
             reads=["sp_acc", "sp_s2"], writes=["sp_acc"])
        S.op("dve", lambda h, k=k: h.tensor_scalar(out=acc[:], in0=acc[:], scalar1=1.0 / (2 * k + 1), scalar2=None, op0=ALU.add),
             reads=["sp_acc"], writes=["sp_acc"])
    S.op("dve", lambda h: h.scalar_tensor_tensor(out=c1, in0=acc[:], scalar=-16.0, in1=s[:], op0=ALU.mult, op1=ALU.mult),
         reads=["sp_acc", "sp_s"], writes=["c1"])
    S.op("dve", lambda h: h.tensor_scalar(out=c2, in0=c1, scalar1=2.0, scalar2=None, op0=ALU.mult),
         reads=["c1"], writes=["c2"])
    C.release(m)


def emit_lru_head(C, c, d, xr, xr_key, XO, prm, tmp, tmpb, init_ap, out_h, out_key, acc_r):
    S = C.S
    (xc, kxc), (r, kr), (ig, kig), (a, ka), (bt, kbt) = tmp
    col = d * 8 + c
    cw = prm["lcw"]

    def xs(k):
        off = (k - 3) if d == 0 else (3 - k)
        return xr[:, XO + off:XO + off + TOWN]

    S.op("dve", lambda h: h.tensor_scalar(out=xc[:, 0:TOWN], in0=xs(0), scalar1=cw[:, d, c, 0:1], scalar2=prm["lcb"][:, col:col + 1],
                                           op0=ALU.mult, op1=ALU.add),
         reads=[xr_key, "consts"], writes=[kxc])
    for k in range(1, 4):
        S.op("dve", lambda h, k=k: h.scalar_tensor_tensor(out=xc[:, 0:TOWN], in0=xs(k), scalar=cw[:, d, c, k:k + 1], in1=xc[:, 0:TOWN],
                                                           op0=ALU.mult, op1=ALU.add),
             reads=[xr_key, kxc, "consts"], writes=[kxc])
    S.op("pool", lambda h: h.tensor_copy(out=tmpb[:], in_=xc[:, 0:TOWN]), reads=[kxc], writes=["tmpb"])
    for ti, (t0, n) in enumerate(tiles_of(TOWN)):
        pa = C.next_ps()
        px = C.next_ps()
        S.op("pe", lambda h, pa=pa, t0=t0, n=n: h.matmul(C.ps[pa][:, 0:n], lhsT=prm["wa"][:, d, c, :], rhs=tmpb[:, t0:t0 + n],
                                                          start=True, stop=True),
             reads=["tmpb", "lruw"], writes=[("ps", pa)])
        S.op("pe", lambda h, px=px, t0=t0, n=n: h.matmul(C.ps[px][:, 0:n], lhsT=prm["wx"][:, d, c, :], rhs=tmpb[:, t0:t0 + n],
                                                          start=True, stop=True),
             reads=["tmpb", "lruw"], writes=[("ps", px)])
        if acc_r is not None:
            S.op("act", lambda h, pa=pa, t0=t0, n=n, ti=ti: h.activation(out=r[:, t0:t0 + n], in_=C.ps[pa][:, 0:n], func=AF.Sigmoid,
                                                                          bias=prm["ba"][:, col:col + 1], accum_out=acc_r[:, ti:ti + 1]),
                 reads=[("ps", pa), "consts"], writes=[kr, "accr"])
        else:
            S.op("act", lambda h, pa=pa, t0=t0, n=n: h.activation(out=r[:, t0:t0 + n], in_=C.ps[pa][:, 0:n], func=AF.Sigmoid,
                                                                   bias=prm["ba"][:, col:col + 1]),
                 reads=[("ps", pa), "consts"], writes=[kr])
        S.op("act", lambda h, px=px, t0=t0, n=n: h.activation(out=ig[:, t0:t0 + n], in_=C.ps[px][:, 0:n], func=AF.Sigmoid,
                                                               bias=prm["bx"][:, col:col + 1]),
             reads=[("ps", px), "consts"], writes=[kig])
    S.op("act", lambda h: h.activation(out=a[:, 0:TOWN], in_=r[:, 0:TOWN], func=AF.Exp, scale=prm["c1"][:, col:col + 1]),
         reads=[kr, "c1"], writes=[ka])
    S.op("act", lambda h: h.activation(out=bt[:, 0:TOWN], in_=r[:, 0:TOWN], func=AF.Exp, scale=prm["c2"][:, col:col + 1]),
         reads=[kr, "c2"], writes=[kbt])
    S.op("act", lambda h: h.activation(out=bt[:, 0:TOWN], in_=bt[:, 0:TOWN], func=AF.Sqrt, scale=-1.0, bias=prm["one"][:, 0:1]),
         reads=[kbt, "consts"], writes=[kbt])
    S.op("dve", lambda h: h.tensor_tensor(out=ig[:, 0:TOWN], in0=ig[:, 0:TOWN], in1=xc[:, 0:TOWN], op=ALU.mult), reads=[kig, kxc], writes=[kig])
    S.op("dve", lambda h: h.tensor_tensor(out=bt[:, 0:TOWN], in0=bt[:, 0:TOWN], in1=ig[:, 0:TOWN], op=ALU.mult), reads=[kbt, kig], writes=[kbt])
    if d == 0:
        S.op("dve", lambda h: h.tensor_tensor_scan(out=out_h, data0=a[:, 0:TOWN], data1=bt[:, 0:TOWN], initial=init_ap,
                                                    op0=ALU.mult, op1=ALU.add),
             reads=[ka, kbt, "carry"], writes=[out_key])
    else:
        S.op("dve", lambda h: h.tensor_tensor_scan(out=out_h[:, ::-1], data0=a[:, TOWN - 1::-1], data1=bt[:, TOWN - 1::-1], initial=init_ap,
                                                    op0=ALU.mult, op1=ALU.add),
             reads=[ka, kbt, "carry"], writes=[out_key])


def load_lru_params(C, prm_d):
    S = C.S
    prm = {}
    prm["lcw"] = C.alloc("lcw", [2, 8, 4], F32)
    for nm in ("lcb", "ba", "bx", "lam", "c1", "c2"):
        prm[nm] = C.alloc(nm, [16], F32)
    prm["one"] = C.alloc("one", [1], F32)
    prm["wa"] = C.alloc("wa", [2, 8, 128], BF16)
    prm["wx"] = C.alloc("wx", [2, 8, 128], BF16)
    S.op("dve", lambda h: h.memset(prm["one"][:], 1.0), writes=["one"])
    S.op("sp", lambda h: h.dma_start(out=prm["lcw"][:], in_=prm_d["lru_conv_w"]), writes=["consts"], dma=True)
    for nm, key in (("lcb", "lru_conv_b"), ("ba", "lru_b_a"), ("bx", "lru_b_x"), ("lam", "lru_lambda")):
        S.op("sp", lambda h, nm=nm, key=key: h.dma_start(out=prm[nm][:], in_=prm_d[key]), writes=["consts"], dma=True)
    S.op("pool", lambda h: h.dma_start(out=prm["wa"][:], in_=prm_d["lru_w_a"]), writes=["lruw"], dma=True)
    S.op("pool", lambda h: h.dma_start(out=prm["wx"][:], in_=prm_d["lru_w_x"]), writes=["lruw"], dma=True)
    emit_softplus_c(C, prm["lam"][:], prm["c1"][:], prm["c2"][:], 16)
    return prm


def _finish(nc, C, block, sems, rings):
    S = C.S
    S.prepare()

    @block.tensor
    def _(h):
        S.run("pe", h, sems, rings)

    @block.scalar
    def _(h):
        S.run("act", h, sems, rings)

    @block.vector
    def _(h):
        S.run("dve", h, sems, rings)

    @block.gpsimd
    def _(h):
        S.run("pool", h, sems, rings)

    @block.sync
    def _(h):
        S.run("sp", h, sems, rings)


def _lru_dram(din):
    return dict(lru_conv_w=din("lru_conv_w", [128, 2, 8, 4]), lru_conv_b=din("lru_conv_b", [128, 16]),
                lru_b_a=din("lru_b_a", [128, 16]), lru_b_x=din("lru_b_x", [128, 16]), lru_lambda=din("lru_lambda", [128, 16]),
                lru_w_a=din("lru_w_a", [128, 2, 8, 128]), lru_w_x=din("lru_w_x", [128, 2, 8, 128]))


def build_A(with_ffn=True):
    nc = bass.Bass("TRN2", target_bir_lowering=False)

    def din(name, shape, dt=F32):
        return nc.dram_tensor(name, shape, dt, kind="ExternalInput").ap()

    def dout(name, shape, dt=F32):
        return nc.dram_tensor(name, shape, dt, kind="ExternalOutput").ap()

    xin = din("xin", [128, DC, TA])
    vmask = din("vmask", [128, TA])
    norms = din("norms", [128, 2 * DC])
    if with_ffn:
        wg = din("wg", [D, DFF]); wu = din("wu", [D, DFF]); wd = din("wd", [DFF, D])
    w_in = din("w_in", [D, DIN])
    pool_w = din("pool_w", [128, 4, 128])
    smallp = din("smallp", [128, 20 + 4 * 31])
    identd = din("ident", [128, 128])
    lru_d = _lru_dram(din)
    xout = dout("xout", [128, DC, TOWN]) if with_ffn else None
    ypc = dout("ypc", [128, 8, TOWN])
    gout = dout("gout", [128, 8, TOWN])
    xrout = dout("xrout", [128, 8, TA])
    car = dout("car", [128, 32])

    with ExitStack() as es:
        C = Ctx(nc, es)
        S = C.S
        sems = {e: es.enter_context(nc.semaphore("s_" + e)) for e in ENGS}
        rings = {e: [es.enter_context(nc.semaphore("r_%s%d" % (e, i))) for i in range(NRING)] for e in ("sp", "pool")}
        block = es.enter_context(nc.Block())

        h_sb = C.alloc("h", [DC, TA], BF16)
        g_sb = C.alloc("g", [2 * DC], F32)
        ones_f = C.alloc("ones_f", [128], F32)
        mx = C.mark()
        x_sb = C.alloc("x", [DC, TA], F32)
        S.op("dve", lambda h: h.memset(ones_f[:], 1.0), writes=["ones"])
        S.op("sp", lambda h: h.dma_start(out=g_sb[:], in_=norms), writes=["consts"], dma=True)
        emit_load_x(C, x_sb, xin, TA)
        if with_ffn:
            emit_rmsnorm(C, x_sb, g_sb, 0, h_sb, TA, ones_f)
            emit_ffn(C, x_sb, h_sb, TA, wg, wu, wd)
            for c in range(0, DC, 4):
                S.op("sp", lambda h, c=c: h.dma_start(out=xout[:, c:c + 4, :], in_=x_sb[:, c:c + 4, HALO:HALO + TOWN]),
                     reads=[("x", cc) for cc in range(c, c + 4)], dma=True)
        emit_rmsnorm(C, x_sb, g_sb, DC, h_sb, TA, ones_f)
        C.release(mx)

        sp_sb = C.alloc("smallp", [20 + 4 * 31], F32)
        S.op("sp", lambda h: h.dma_start(out=sp_sb[:], in_=smallp), writes=["consts"], dma=True)
        pw_sb = C.alloc("pool_w", [4, 128], BF16)
        S.op("pool", lambda h: h.dma_start(out=pw_sb[:], in_=pool_w), writes=["poolw"], dma=True)
        prm = load_lru_params(C, lru_d)
        vm = C.alloc("vm", [TA], F32)
        S.op("sp", lambda h: h.dma_start(out=vm[:], in_=vmask), writes=["vm"], dma=True)
        ones_t = C.alloc("ones_t", [TA], F32)
        S.op("pool", lambda h: h.memset(ones_t[:], 1.0), writes=["ones_t"])
        csm = C.alloc("csm", [TA], F32)
        csu = C.alloc("csu", [TA], F32)
        S.op("dve", lambda h: h.tensor_tensor_scan(out=csm[:], data0=ones_t[:], data1=vm[:], initial=0.0, op0=ALU.mult, op1=ALU.add),
             reads=["ones_t", "vm"], writes=["csm"])
        ident = C.alloc("ident", [128], BF16)
        identf = C.alloc("identf", [128], F32)
        S.op("sp", lambda h: h.dma_start(out=identf[:], in_=identd), writes=["identf"], dma=True)
        S.op("pool", lambda h: h.tensor_copy(out=ident[:], in_=identf[:]), reads=["identf"], writes=["ident"])

        NWR = 2
        wib = [C.alloc("wib%d" % i, [DC, 256], BF16) for i in range(NWR)]
        NU = 4
        ub = [C.alloc("ub%d" % i, [TA], F32) for i in range(NU)]
        tmpf = [C.alloc("tf%d" % i, [TA], F32) for i in range(6)]
        tmpb = C.alloc("tb", [TOWN], BF16)
        zb = C.alloc("zb", [4, TA], BF16)
        z2 = C.alloc("z2", [4, TOWN], F32)
        diag = C.alloc("diag", [31, 128], BF16)
        stg = [C.alloc("stg%d" % i, [TOWN], F32) for i in range(2)]
        accr = C.alloc("accr", [16, 2], F32)
        car_sb = C.alloc("car_sb", [32], F32)
        w_inv = w_in.rearrange("(kc p) n -> p kc n", p=128)
        TT = tiles_of(TA)
        st = {"w": 0, "u": 0, "stg": 0}

        def proj_pair(cols):
            wi = st["w"] % NWR
            st["w"] += 1
            for q, cc in enumerate(cols):
                S.op("pool", lambda h, q=q, cc=cc, wi=wi: h.dma_start(out=wib[wi][:, :, q * 128:(q + 1) * 128],
                                                                     in_=w_inv[:, :, cc * 128:(cc + 1) * 128]),
                     writes=[("wib", wi, q)], dma=True)
            res = []
            for q, cc in enumerate(cols):
                ui = st["u"] % NU
                st["u"] += 1
                res.append(ui)
                for (t0, n) in TT:
                    pi = C.next_ps()

                    def mm(h, q=q, wi=wi, pi=pi, t0=t0, n=n):
                        for kc in range(DC):
                            ins = h.matmul(C.ps[pi][:, 0:n], lhsT=wib[wi][:, kc, q * 128:(q + 1) * 128], rhs=h_sb[:, kc, t0:t0 + n],
                                           start=(kc == 0), stop=(kc == DC - 1))
                        return ins

                    S.op("pe", mm, reads=[("wib", wi, q)] + [("h", c) for c in range(DC)], writes=[("ps", pi)])
                    S.op("act", lambda h, ui=ui, pi=pi, t0=t0, n=n: h.activation(out=ub[ui][:, t0:t0 + n], in_=C.ps[pi][:, 0:n], func=AF.Copy),
                         reads=[("ps", pi)], writes=[("ub", ui)])
            return res

        def store(dst_ap, src_key, src_ap):
            S.op("sp", lambda h: h.dma_start(out=dst_ap, in_=src_ap), reads=[src_key], dma=True)

        WINS = (2, 4, 8, 16)
        for gp in range(2):
            uis = proj_pair([2 * gp, 2 * gp + 1])
            for q in range(2):
                g = 2 * gp + q
                w2 = WINS[g] // 2
                ui = uis[q]
                ssum, rc = tmpf[1], tmpf[2]
                S.op("dve", lambda h, ui=ui: h.tensor_tensor_scan(out=csu[:], data0=ones_t[:], data1=ub[ui][:], initial=0.0,
                                                                   op0=ALU.mult, op1=ALU.add),
                     reads=["ones_t", ("ub", ui)], writes=["csu"])
                lo = HALO - w2 - 1
                hi = HALO + w2 - 1
                S.op("dve", lambda h, lo=lo, hi=hi: h.tensor_tensor(out=ssum[:, 0:TOWN], in0=csu[:, hi:hi + TOWN], in1=csu[:, lo:lo + TOWN], op=ALU.subtract),
                     reads=["csu"], writes=[("tf", 1)])
                S.op("pool", lambda h, lo=lo, hi=hi: h.tensor_tensor(out=rc[:, 0:TOWN], in0=csm[:, hi:hi + TOWN], in1=csm[:, lo:lo + TOWN], op=ALU.subtract),
                     reads=["csm"], writes=[("tf", 2)])
                S.op("dve", lambda h: h.reciprocal(out=rc[:, 0:TOWN], in_=rc[:, 0:TOWN]), reads=[("tf", 2)], writes=[("tf", 2)])
                S.op("dve", lambda h: h.tensor_tensor(out=ssum[:, 0:TOWN], in0=ssum[:, 0:TOWN], in1=rc[:, 0:TOWN], op=ALU.mult),
                     reads=[("tf", 1), ("tf", 2)], writes=[("tf", 1)])
                S.op("dve", lambda h, ui=ui: h.tensor_tensor(out=tmpb[:], in0=ssum[:, 0:TOWN], in1=ub[ui][:, HALO:HALO + TOWN], op=ALU.subtract),
                     reads=[("tf", 1), ("ub", ui)], writes=["tmpb"])
                si = st["stg"] % 2
                st["stg"] += 1
                for (t0, n) in tiles_of(TOWN):
                    pi = C.next_ps()
                    S.op("pe", lambda h, g=g, pi=pi, t0=t0, n=n: h.matmul(C.ps[pi][:, 0:n], lhsT=pw_sb[:, g, :], rhs=tmpb[:, t0:t0 + n], start=True, stop=True),
                         reads=["tmpb", "poolw"], writes=[("ps", pi)])
                    S.op("act", lambda h, g=g, pi=pi, si=si, t0=t0, n=n: h.activation(out=stg[si][:, t0:t0 + n], in_=C.ps[pi][:, 0:n], func=AF.Copy,
                                                                                       scale=sp_sb[:, g:g + 1]),
                         reads=[("ps", pi), "consts"], writes=[("stg", si)])
                store(ypc[:, g, :], ("stg", si), stg[si][:])

        for c in range(4):
            uv, ug = proj_pair([4 + c, 8 + c])
            S.op("act", lambda h, ug=ug: h.activation(out=ub[ug][:], in_=ub[ug][:], func=AF.Sigmoid), reads=[("ub", ug)], writes=[("ub", ug)])
            S.op("dve", lambda h, c=c, uv=uv, ug=ug: h.tensor_tensor(out=zb[:, c, :], in0=ub[uv][:], in1=ub[ug][:], op=ALU.mult),
                 reads=[("ub", uv), ("ub", ug)], writes=[("zb", c)])
            for k in range(31):
                eng = "dve" if k % 2 == 0 else "pool"
                S.op(eng, lambda h, c=c, k=k: h.tensor_scalar(out=diag[:, k, :], in0=ident[:], scalar1=sp_sb[:, 20 + c * 31 + k:20 + c * 31 + k + 1],
                                                              scalar2=None, op0=ALU.mult),
                     reads=["ident", "consts"], writes=[("diag", k)])
            for (t0, n) in tiles_of(TOWN):
                pi = C.next_ps()

                def mmc(h, c=c, pi=pi, t0=t0, n=n):
                    for k in range(31):
                        ins = h.matmul(C.ps[pi][:, 0:n], lhsT=diag[:, k, :], rhs=zb[:, c, t0 + 1 + k:t0 + 1 + k + n], start=(k == 0), stop=(k == 30))
                    return ins

                S.op("pe", mmc, reads=[("zb", c)] + [("diag", k) for k in range(31)], writes=[("ps", pi)])
                S.op("act", lambda h, c=c, pi=pi, t0=t0, n=n: h.activation(out=z2[:, c, t0:t0 + n], in_=C.ps[pi][:, 0:n], func=AF.Identity,
                                                                          bias=sp_sb[:, 4 + c:5 + c]),
                     reads=[("ps", pi), "consts"], writes=[("z2", c, t0)])
        for (t0, n) in tiles_of(TOWN):
            pm = C.next_ps()
            pq = C.next_ps()
            for c in range(4):
                S.op("pe", lambda h, c=c, pm=pm, t0=t0, n=n: h.matmul(C.ps[pm][:, 0:n], lhsT=ones_f[:], rhs=z2[:, c, t0:t0 + n], start=(c == 0), stop=(c == 3)),
                     reads=[("z2", c, t0), "ones"], writes=[("ps", pm)])
            for c in range(4):
                S.op("act", lambda h, c=c, t0=t0, n=n: h.activation(out=tmpf[c][:, 0:n], in_=z2[:, c, t0:t0 + n], func=AF.Square),
                     reads=[("z2", c, t0)], writes=[("tf", c)])
                S.op("pe", lambda h, c=c, pq=pq, n=n: h.matmul(C.ps[pq][:, 0:n], lhsT=ones_f[:], rhs=tmpf[c][:, 0:n], start=(c == 0), stop=(c == 3)),
                     reads=[("tf", c), "ones"], writes=[("ps", pq)])
            mean, var = tmpf[4], tmpf[5]
            S.op("dve", lambda h, pm=pm, n=n: h.tensor_scalar(out=mean[:, 0:n], in0=C.ps[pm][:, 0:n], scalar1=1.0 / 512, scalar2=None, op0=ALU.mult),
                 reads=[("ps", pm)], writes=[("tf", 4)])
            S.op("dve", lambda h, pq=pq, n=n: h.tensor_scalar(out=var[:, 0:n], in0=C.ps[pq][:, 0:n], scalar1=1.0 / 512, scalar2=LN_EPS, op0=ALU.mult, op1=ALU.add),
                 reads=[("ps", pq)], writes=[("tf", 5)])
            S.op("dve", lambda h, n=n: h.tensor_tensor(out=tmpf[0][:, 0:n], in0=mean[:, 0:n], in1=mean[:, 0:n], op=ALU.mult),
                 reads=[("tf", 4)], writes=[("tf", 0)])
            S.op("dve", lambda h, n=n: h.tensor_tensor(out=var[:, 0:n], in0=var[:, 0:n], in1=tmpf[0][:, 0:n], op=ALU.subtract),
                 reads=[("tf", 5), ("tf", 0)], writes=[("tf", 5)])
            S.op("act", lambda h, n=n: h.activation(out=var[:, 0:n], in_=var[:, 0:n], func=AF.Sqrt), reads=[("tf", 5)], writes=[("tf", 5)])
            S.op("dve", lambda h, n=n: h.reciprocal(out=var[:, 0:n], in_=var[:, 0:n]), reads=[("tf", 5)], writes=[("tf", 5)])
            for c in range(4):
                t1 = tmpf[c % 2]
                S.op("dve", lambda h, c=c, t1=t1, t0=t0, n=n: h.tensor_tensor(out=t1[:, 0:n], in0=z2[:, c, t0:t0 + n], in1=mean[:, 0:n], op=ALU.subtract),
                     reads=[("z2", c, t0), ("tf", 4)], writes=[("tf", c % 2)])
                S.op("dve", lambda h, c=c, t1=t1, n=n: h.tensor_tensor(out=t1[:, 0:n], in0=t1[:, 0:n], in1=var[:, 0:n], op=ALU.mult),
                     reads=[("tf", c % 2), ("tf", 5)], writes=[("tf", c % 2)])
                S.op("act", lambda h, c=c, t1=t1, t0=t0, n=n: h.activation(out=z2[:, c, t0:t0 + n], in_=t1[:, 0:n], func=AF.Silu,
                                                                          scale=sp_sb[:, 8 + c:9 + c], bias=sp_sb[:, 12 + c:13 + c]),
                     reads=[("tf", c % 2), "consts"], writes=[("z2", c, t0)])
        for c in range(4):
            S.op("sp", lambda h, c=c: h.dma_start(out=ypc[:, 4 + c, :], in_=z2[:, c, :]),
                 reads=[("z2", c, t0) for (t0, n) in tiles_of(TOWN)], dma=True)

        for gp in range(4):
            uis = proj_pair([12 + 2 * gp, 13 + 2 * gp])
            for q in range(2):
                c = 2 * gp + q
                ui = uis[q]
                t1 = tmpf[0][:, 0:TOWN]
                si = st["stg"] % 2
                st["stg"] += 1
                uo = ub[ui][:, HALO:HALO + TOWN]
                S.op("pool", lambda h, uo=uo, t1=t1: h.tensor_tensor(out=t1, in0=uo, in1=uo, op=ALU.mult), reads=[("ub", ui)], writes=[("tf", 0)])
                S.op("dve", lambda h, t1=t1: h.tensor_scalar(out=t1, in0=t1, scalar1=0.044715, scalar2=1.0, op0=ALU.mult, op1=ALU.add),
                     reads=[("tf", 0)], writes=[("tf", 0)])
                S.op("dve", lambda h, uo=uo, t1=t1: h.tensor_tensor(out=t1, in0=t1, in1=uo, op=ALU.mult), reads=[("tf", 0), ("ub", ui)], writes=[("tf", 0)])
                S.op("act", lambda h, t1=t1: h.activation(out=t1, in_=t1, func=AF.Sigmoid, scale=1.5957691216057308), reads=[("tf", 0)], writes=[("tf", 0)])
                S.op("dve", lambda h, uo=uo, si=si, t1=t1: h.tensor_tensor(out=stg[si][:], in0=t1, in1=uo, op=ALU.mult),
                     reads=[("tf", 0), ("ub", ui)], writes=[("stg", si)])
                store(gout[:, c, :], ("stg", si), stg[si][:])

        S.op("dve", lambda h: h.memset(accr[:], 0.0), writes=["accr"])
        tmp5 = [(tmpf[i], ("tf", i)) for i in range(5)]
        hout = tmpf[5][:, 0:TOWN]
        for gp in range(4):
            uis = proj_pair([20 + 2 * gp, 21 + 2 * gp])
            for q in range(2):
                c = 2 * gp + q
                ui = uis[q]
                store(xrout[:, c, :], ("ub", ui), ub[ui][:])
                for d in range(2):
                    col = d * 8 + c
                    emit_lru_head(C, c, d, ub[ui], ("ub", ui), HALO, prm, tmp5, tmpb, 0.0, hout, ("tf", 5), accr[:, col, :])
                    last = TOWN - 1 if d == 0 else 0
                    S.op("pool", lambda h, col=col, last=last: h.tensor_copy(out=car_sb[:, 16 + col:17 + col], in_=hout[:, last:last + 1]),
                         reads=[("tf", 5)], writes=["car"])
        S.op("dve", lambda h: h.tensor_tensor(out=car_sb[:, 0:16], in0=accr[:, :, 0], in1=accr[:, :, 1], op=ALU.add),
             reads=["accr"], writes=["car"])
        S.op("sp", lambda h: h.dma_start(out=car, in_=car_sb[:]), reads=["car"], dma=True)
        _finish(nc, C, block, sems, rings)
    return nc


def build_B(final, next_ffn=False):
    nc = bass.Bass("TRN2", target_bir_lowering=False)

    def din(name, shape, dt=F32):
        return nc.dram_tensor(name, shape, dt, kind="ExternalInput").ap()

    def dout(name, shape, dt=F32):
        return nc.dram_tensor(name, shape, dt, kind="ExternalOutput").ap()

    xin = din("xin", [128, DC, TOWN])
    ypc = din("ypc", [128, 8, TOWN])
    gin = din("gin", [128, 8, TOWN])
    xrin = din("xrin", [128, 8, TA])
    carall = din("carall", [128, NCORES, 32])
    onehot = din("onehot", [128, NCORES])
    norms = din("norms", [128, 3 * DC])
    w_out = din("w_out", [D, D])
    wg = din("wg", [D, DFF]); wu = din("wu", [D, DFF]); wd = din("wd", [DFF, D])
    if next_ffn:
        wgn = din("wgn", [D, DFF]); wun = din("wun", [D, DFF]); wdn = din("wdn", [DFF, D])
    lru_d = _lru_dram(din)
    xout = dout("xout", [128, DC, TOWN])

    with ExitStack() as es:
        C = Ctx(nc, es)
        S = C.S
        sems = {e: es.enter_context(nc.semaphore("s_" + e)) for e in ENGS}
        rings = {e: [es.enter_context(nc.semaphore("r_%s%d" % (e, i))) for i in range(NRING)] for e in ("sp", "pool")}
        block = es.enter_context(nc.Block())

        x_sb = C.alloc("x", [DC, TOWN], F32)
        g_sb = C.alloc("g", [3 * DC], F32)
        ones_f = C.alloc("ones_f", [128], F32)
        S.op("dve", lambda h: h.memset(ones_f[:], 1.0), writes=["ones"])
        S.op("sp", lambda h: h.dma_start(out=g_sb[:], in_=norms), writes=["consts"], dma=True)
        m_y = C.mark()
        y_sb = C.alloc("y", [DC, TOWN], BF16)
        for c in range(0, 8, 4):
            S.op("pool", lambda h, c=c: h.dma_start(out=y_sb[:, c:c + 4, :], in_=ypc[:, c:c + 4, :]),
                 writes=[("y", cc) for cc in range(c, c + 4)], dma=True)
        m_l = C.mark()
        prm = load_lru_params(C, lru_d)
        ca = C.alloc("carall", [NCORES, 32], F32)
        oh = C.alloc("onehot", [NCORES], F32)
        S.op("sp", lambda h: h.dma_start(out=ca[:], in_=carall), writes=["carall"], dma=True)
        S.op("sp", lambda h: h.dma_start(out=oh[:], in_=onehot), writes=["consts"], dma=True)
        at = C.alloc("atot", [NCORES, 16], F32)
        for j in range(NCORES):
            S.op("dve", lambda h, j=j: h.tensor_tensor(out=at[:, j, :], in0=ca[:, j, 0:16], in1=prm["c1"][:], op=ALU.mult),
                 reads=["carall", "c1"], writes=[("at", j)])
            S.op("act", lambda h, j=j: h.activation(out=at[:, j, :], in_=at[:, j, :], func=AF.Exp), reads=[("at", j)], writes=[("at", j)])
        cin = C.alloc("cin", [NCORES, 16], F32)
        S.op("dve", lambda h: h.memset(cin[:], 0.0), writes=["cin"])
        for j in range(1, NCORES):
            S.op("dve", lambda h, j=j: h.tensor_tensor(out=cin[:, j, 0:8], in0=at[:, j - 1, 0:8], in1=cin[:, j - 1, 0:8], op=ALU.mult),
                 reads=[("at", j - 1), "cin"], writes=["cin"])
            S.op("dve", lambda h, j=j: h.tensor_tensor(out=cin[:, j, 0:8], in0=cin[:, j, 0:8], in1=ca[:, j - 1, 16:24], op=ALU.add),
                 reads=["cin", "carall"], writes=["cin"])
        for j in range(NCORES - 2, -1, -1):
            S.op("dve", lambda h, j=j: h.tensor_tensor(out=cin[:, j, 8:16], in0=at[:, j + 1, 8:16], in1=cin[:, j + 1, 8:16], op=ALU.mult),
                 reads=[("at", j + 1), "cin"], writes=["cin"])
            S.op("dve", lambda h, j=j: h.tensor_tensor(out=cin[:, j, 8:16], in0=cin[:, j, 8:16], in1=ca[:, j + 1, 24:32], op=ALU.add),
                 reads=["cin", "carall"], writes=["cin"])
        mine = C.alloc("mine", [16], F32)
        S.op("dve", lambda h: h.tensor_scalar(out=mine[:], in0=cin[:, 0, :], scalar1=oh[:, 0:1], scalar2=None, op0=ALU.mult),
             reads=["cin", "consts"], writes=["carry"])
        for j in range(1, NCORES):
            S.op("dve", lambda h, j=j: h.scalar_tensor_tensor(out=mine[:], in0=cin[:, j, :], scalar=oh[:, j:j + 1], in1=mine[:],
                                                               op0=ALU.mult, op1=ALU.add),
                 reads=["cin", "consts", "carry"], writes=["carry"])

        xrb = [C.alloc("xrb%d" % i, [TA], F32) for i in range(2)]
        gb = [C.alloc("gb%d" % i, [TOWN], F32) for i in range(2)]
        tmpf = [C.alloc("tf%d" % i, [TA], F32) for i in range(5)]
        tmpb = C.alloc("tb", [TOWN], BF16)
        hf = C.alloc("hf", [TOWN], F32)
        hb = C.alloc("hb", [TOWN], F32)
        tmp5 = [(tmpf[i], ("tf", i)) for i in range(5)]
        for c in range(8):
            bi = c % 2
            S.op("sp", lambda h, c=c, bi=bi: h.dma_start(out=xrb[bi][:], in_=xrin[:, c, :]), writes=[("xrb", bi)], dma=True)
            S.op("sp", lambda h, c=c, bi=bi: h.dma_start(out=gb[bi][:], in_=gin[:, c, :]), writes=[("gb", bi)], dma=True)
            if c == 1:
                emit_load_x(C, x_sb, xin, TOWN)
            emit_lru_head(C, c, 0, xrb[bi], ("xrb", bi), HALO, prm, tmp5, tmpb, mine[:, c:c + 1], hf[:], "hf", None)
            emit_lru_head(C, c, 1, xrb[bi], ("xrb", bi), HALO, prm, tmp5, tmpb, mine[:, 8 + c:9 + c], hb[:], "hb", None)
            S.op("dve", lambda h: h.tensor_tensor(out=hf[:], in0=hf[:], in1=hb[:], op=ALU.add), reads=["hf", "hb"], writes=["hf"])
            S.op("dve", lambda h, c=c, bi=bi: h.tensor_tensor(out=y_sb[:, 8 + c, :], in0=hf[:], in1=gb[bi][:], op=ALU.mult),
                 reads=["hf", ("gb", bi)], writes=[("y", 8 + c)])
        C.release(m_l)

        NWR = 3
        wob = [C.alloc("wob%d" % i, [DC, 256], BF16) for i in range(NWR)]
        w_ov = w_out.rearrange("(kc p) n -> p kc n", p=128)
        for n2 in range(DC // 2):
            wi = n2 % NWR
            S.op("pool", lambda h, n2=n2, wi=wi: h.dma_start(out=wob[wi][:], in_=w_ov[:, :, n2 * 256:(n2 + 1) * 256]),
                 writes=[("wob", wi)], dma=True)
            for nn in range(2):
                nchunk = 2 * n2 + nn
                for (t0, n) in tiles_of(TOWN):
                    po = C.next_ps()

                    def mmo(h, wi=wi, nn=nn, po=po, t0=t0, n=n):
                        for kc in range(DC):
                            ins = h.matmul(C.ps[po][:, 0:n], lhsT=wob[wi][:, kc, nn * 128:(nn + 1) * 128], rhs=y_sb[:, kc, t0:t0 + n],
                                           start=(kc == 0), stop=(kc == DC - 1))
                        return ins

                    S.op("pe", mmo, reads=[("wob", wi)] + [("y", kc) for kc in range(DC)], writes=[("ps", po)])
                    S.op("dve", lambda h, po=po, nchunk=nchunk, t0=t0, n=n: h.tensor_tensor(
                        out=x_sb[:, nchunk, t0:t0 + n], in0=C.ps[po][:, 0:n], in1=x_sb[:, nchunk, t0:t0 + n], op=ALU.add),
                        reads=[("ps", po), ("x", nchunk)], writes=[("x", nchunk)])
        C.release(m_y)

        h_sb = C.alloc("h", [DC, TOWN], BF16)
        emit_rmsnorm(C, x_sb, g_sb, 0, h_sb, TOWN, ones_f)
        emit_ffn(C, x_sb, h_sb, TOWN, wg, wu, wd)
        if next_ffn:
            emit_rmsnorm(C, x_sb, g_sb, 2 * DC, h_sb, TOWN, ones_f)
            emit_ffn(C, x_sb, h_sb, TOWN, wgn, wun, wdn)
        if final:
            C.release(C.mark())
            C.off = m_y
            o_sb = C.alloc("o", [DC, TOWN], F32)
            emit_rmsnorm(C, x_sb, g_sb, DC, None, TOWN, ones_f, hkey="o", out_f32=o_sb)
            for c in range(0, DC, 4):
                S.op("sp", lambda h, c=c: h.dma_start(out=xout[:, c:c + 4, :], in_=o_sb[:, c:c + 4, :]),
                     reads=[("o", cc) for cc in range(c, c + 4)], dma=True)
        else:
            for c in range(0, DC, 4):
                S.op("sp", lambda h, c=c: h.dma_start(out=xout[:, c:c + 4, :], in_=x_sb[:, c:c + 4, :]),
                     reads=[("x", cc) for cc in range(c, c + 4)], dma=True)
        _finish(nc, C, block, sems, rings)
    return nc


_CACHE = {}


def _prog(name):
    if name not in _CACHE:
        _CACHE[name] = {"A": lambda: build_A(True), "An": lambda: build_A(False), "Bn": lambda: build_B(False, True),
                        "Bf": lambda: build_B(True, False)}[name]()
    return _CACHE[name]


def _fm(v):
    v = np.asarray(v, dtype=np.float32)
    return np.ascontiguousarray(v.reshape(-1, 128).T)


def _run(nc, in_maps):
    res = run_bass_kernel_spmd(nc, in_maps, core_ids=list(range(NCORES)))
    return res.results


def kernel(x, norm_ffn1, ffn1_w_gate, ffn1_w_up, ffn1_w_down, norm_mix, w_in,
           pool_w, pool_scale, conv_dw_w, conv_dw_b, conv_ln_g, conv_ln_b,
           lru_conv_w, lru_conv_b, lru_w_a, lru_b_a, lru_w_x, lru_b_x, lru_lambda,
           w_out, norm_ffn2, ffn2_w_gate, ffn2_w_up, ffn2_w_down, norm_final, _nlayers=DEPTH, _debug=None):
    f32 = np.float32
    A = lambda a: np.ascontiguousarray(np.asarray(a, dtype=f32))
    x = A(x)
    xfm = np.ascontiguousarray(x.reshape(SEQ, DC, 128).transpose(2, 1, 0))
    ident = np.eye(128, dtype=f32)
    vfull = np.zeros((SEQ + 2 * HALO,), f32)
    vfull[HALO:HALO + SEQ] = 1.0
    onehots = [np.ascontiguousarray(np.broadcast_to(np.eye(NCORES, dtype=f32)[c][None, :], (128, NCORES))) for c in range(NCORES)]
    for l in range(_nlayers):
        lp = dict(
            lru_conv_w=np.ascontiguousarray(A(lru_conv_w[l]).reshape(2, 4, 8, 128).transpose(3, 0, 2, 1)),
            lru_conv_b=_fm(lru_conv_b[l]), lru_b_a=_fm(lru_b_a[l]), lru_b_x=_fm(lru_b_x[l]), lru_lambda=_fm(lru_lambda[l]),
            lru_w_a=np.ascontiguousarray(A(lru_w_a[l]).transpose(2, 0, 1, 3)),
            lru_w_x=np.ascontiguousarray(A(lru_w_x[l]).transpose(2, 0, 1, 3)),
        )
        smallp = np.zeros((128, 20 + 4 * 31), f32)
        smallp[:, 0:4] = _fm(pool_scale[l])
        smallp[:, 4:8] = _fm(conv_dw_b[l])
        smallp[:, 8:12] = _fm(conv_ln_g[l])
        smallp[:, 12:16] = _fm(conv_ln_b[l])
        smallp[:, 20:] = A(conv_dw_w[l]).reshape(31, 4, 128).transpose(2, 1, 0).reshape(128, 124)
        commonA = dict(norms=np.concatenate([_fm(norm_ffn1[l]), _fm(norm_mix[l])], axis=1), w_in=A(w_in[l]),
                       pool_w=np.ascontiguousarray(A(pool_w[l]).transpose(1, 0, 2)), smallp=smallp, ident=ident, **lp)
        if l == 0:
            commonA.update(wg=A(ffn1_w_gate[l]), wu=A(ffn1_w_up[l]), wd=A(ffn1_w_down[l]))
        xpad = np.zeros((128, DC, SEQ + 2 * HALO), f32)
        xpad[:, :, HALO:HALO + SEQ] = xfm
        in_maps = []
        for c in range(NCORES):
            s = c * TOWN
            in_maps.append(dict(xin=np.ascontiguousarray(xpad[:, :, s:s + TA]),
                                vmask=np.ascontiguousarray(np.broadcast_to(vfull[s:s + TA][None, :], (128, TA))), **commonA))
        ra = _run(_prog("A" if l == 0 else "An"), in_maps)
        if _debug is not None:
            _debug["A%d" % l] = ra
        carall = np.ascontiguousarray(np.stack([ra[c]["car"] for c in range(NCORES)], axis=1))
        last = (l == DEPTH - 1)
        nl1 = min(l + 1, DEPTH - 1)
        commonB = dict(norms=np.concatenate([_fm(norm_ffn2[l]), _fm(norm_final), _fm(norm_ffn1[nl1])], axis=1), w_out=A(w_out[l]),
                       wg=A(ffn2_w_gate[l]), wu=A(ffn2_w_up[l]), wd=A(ffn2_w_down[l]), carall=carall, **lp)
        if not last:
            commonB.update(wgn=A(ffn1_w_gate[l + 1]), wun=A(ffn1_w_up[l + 1]), wdn=A(ffn1_w_down[l + 1]))
        in_maps = []
        for c in range(NCORES):
            xin_b = ra[c]["xout"] if l == 0 else np.ascontiguousarray(xfm[:, :, c * TOWN:(c + 1) * TOWN])
            in_maps.append(dict(xin=xin_b, ypc=ra[c]["ypc"], gin=ra[c]["gout"], xrin=ra[c]["xrout"],
                                onehot=onehots[c], **commonB))
        rb = _run(_prog("Bf" if last else "Bn"), in_maps)
        if _debug is not None:
            _debug["B%d" % l] = rb
        xfm = np.concatenate([rb[c]["xout"] for c in range(NCORES)], axis=2)
    out = np.ascontiguousarray(xfm.transpose(2, 1, 0)).reshape(1, SEQ, D)
    return out.astype(np.float32)
```
